# Optimizing a Trainium2 kernel written in Bass

```python
import math
import jax, jax.numpy as jnp
from jax import lax
import numpy as np


D_MODEL = 1024
BATCH = 2
SEQ = 16384
DEPTH = 2
DEC_BATCH = 8
DEC_SEQ = 2048
PAST_LEN = 128

N_MIXERS = 2
N_A_LAYERS = (DEPTH + N_MIXERS - 1) // N_MIXERS
N_B_LAYERS = DEPTH // N_MIXERS
S5_GROUP = 16
S5_GROUPS = D_MODEL // S5_GROUP
S5_STATE = 64
S5_CHUNK = 128
GLA_HEADS = 4
GLA_DK = D_MODEL // 2
GLA_DV = D_MODEL
GLA_HEAD_K = GLA_DK // GLA_HEADS
GLA_HEAD_V = GLA_DV // GLA_HEADS
GLA_GATE_RANK = 16
GLA_GATE_TAU = 16.0
GLA_CHUNK = 64
D_FF = 4 * D_MODEL
N_MOD = 6
EPS = 1e-6

kernel_name = 'hybrid_s5_gla_encoder'


def rmsnorm(x, g):
    xf = x.astype(jnp.float32)
    y = xf * lax.rsqrt(jnp.mean(xf * xf, axis=-1, keepdims=True) + EPS)
    return (y * g.astype(jnp.float32)).astype(x.dtype)


def _cplx_combine(left, right):
    a1r, a1i, b1r, b1i = left
    a2r, a2i, b2r, b2i = right
    return (a2r * a1r - a2i * a1i, a2r * a1i + a2i * a1r,
            a2r * b1r - a2i * b1i + b2r, a2r * b1i + a2i * b1r + b2i)


def s5_direction(u, lam_re, lam_im, log_dt, b_re, b_im, c_re, c_im):
    f32 = jnp.float32
    bsz, L = u.shape[0], u.shape[1]
    dt = jnp.exp(log_dt.astype(f32))[:, None]
    lr, li = lam_re.astype(f32), lam_im.astype(f32)
    mag = jnp.exp(lr * dt)
    ab_re, ab_im = mag * jnp.cos(li * dt), mag * jnp.sin(li * dt)
    den = lr * lr + li * li
    z_re = ((ab_re - 1.0) * lr + ab_im * li) / den
    z_im = (ab_im * lr - (ab_re - 1.0) * li) / den
    br, bi = b_re.astype(f32), b_im.astype(f32)
    bb_re = z_re[..., None] * br - z_im[..., None] * bi
    bb_im = z_re[..., None] * bi + z_im[..., None] * br
    cr, ci = c_re.astype(f32), c_im.astype(f32)
    n_chunks = L // S5_CHUNK
    uc = u.astype(f32).reshape(bsz, n_chunks, S5_CHUNK, S5_GROUPS, S5_GROUP).transpose(1, 0, 2, 3, 4)
    a_re = jnp.broadcast_to(ab_re, (bsz, S5_CHUNK, S5_GROUPS, S5_STATE))
    a_im = jnp.broadcast_to(ab_im, (bsz, S5_CHUNK, S5_GROUPS, S5_STATE))

    def step(carry, u_c):
        h_re, h_im = carry
        bu_re = jnp.einsum('bcgh,gph->bcgp', u_c, bb_re)
        bu_im = jnp.einsum('bcgh,gph->bcgp', u_c, bb_im)
        pr, pim, sr, si = lax.associative_scan(_cplx_combine, (a_re, a_im, bu_re, bu_im), axis=1)
        hr = pr * h_re[:, None] - pim * h_im[:, None] + sr
        hi = pr * h_im[:, None] + pim * h_re[:, None] + si
        y = jnp.einsum('bcgp,ghp->bcgh', hr, cr) - jnp.einsum('bcgp,ghp->bcgh', hi, ci)
        return (hr[:, -1], hi[:, -1]), y

    h0 = jnp.zeros((bsz, S5_GROUPS, S5_STATE), f32)
    _, ys = lax.scan(step, (h0, h0), uc)
    return ys.transpose(1, 0, 2, 3, 4).reshape(bsz, L, S5_GROUPS, S5_GROUP)


def s5_mixer(h, lam_re, lam_im, log_dt, b_re, b_im, c_re, c_im, d_skip, w_glu):
    bsz, L, _ = h.shape
    u = h.reshape(bsz, L, S5_GROUPS, S5_GROUP)
    y_f = s5_direction(u, lam_re[0], lam_im[0], log_dt[0], b_re[0], b_im[0], c_re[0], c_im[0])
    y_b = s5_direction(u[:, ::-1], lam_re[1], lam_im[1], log_dt[1], b_re[1], b_im[1], c_re[1], c_im[1])[:, ::-1]
    y = (y_f + y_b).reshape(bsz, L, D_MODEL).astype(h.dtype) + d_skip * h
    g = jax.nn.gelu(y)
    val, gate = jnp.split(g @ w_glu, 2, axis=-1)
    return val * jax.nn.sigmoid(gate)


def gla_direction(q, k, v, g):
    bsz, H, L, dk = q.shape
    dv = v.shape[-1]
    n = L // GLA_CHUNK

    def chunks(t):
        return t.reshape(bsz, H, n, GLA_CHUNK, t.shape[-1]).transpose(2, 0, 1, 3, 4)

    mask = jnp.tril(jnp.ones((GLA_CHUNK, GLA_CHUNK), dtype=bool))

    def step(S, inp):
        qc, kc, vc, gc = inp
        bcum = jnp.cumsum(gc, axis=2)
        b_last = bcum[:, :, -1:]
        q_dec = qc * jnp.exp(bcum)
        k_dec = kc * jnp.exp(-bcum)
        scores = jnp.where(mask, jnp.einsum('bhcd,bhsd->bhcs', q_dec, k_dec), 0.0)
        o = jnp.einsum('bhcs,bhsv->bhcv', scores, vc) + jnp.einsum('bhcd,bhdv->bhcv', q_dec, S)
        k_tail = kc * jnp.exp(b_last - bcum)
        S_new = jnp.exp(b_last[:, :, 0])[..., None] * S + jnp.einsum('bhcd,bhcv->bhdv', k_tail, vc)
        return S_new, o

    S0 = jnp.zeros((bsz, H, dk, dv), jnp.float32)
    _, o = lax.scan(step, S0, (chunks(q), chunks(k), chunks(v), chunks(g)))
    return o.transpose(1, 2, 0, 3, 4).reshape(bsz, H, L, dv)


def gla_mixer(h, w_in, w_a1, w_a2, b_a, norm_g, w_out):
    f32 = jnp.float32
    bsz, L, _ = h.shape
    proj = h @ w_in
    q, k, v, r = jnp.split(proj, [GLA_DK, 2 * GLA_DK, 2 * GLA_DK + GLA_DV], axis=-1)

    def heads(t, d):
        return t.reshape(bsz, L, GLA_HEADS, d).transpose(0, 2, 1, 3).astype(f32)

    q = heads(q, GLA_HEAD_K) * (GLA_HEAD_K ** -0.5)
    k = heads(k, GLA_HEAD_K)
    v = heads(v, GLA_HEAD_V)

    def log_gate(j):
        z = (h @ w_a1[j]) @ w_a2[j] + b_a[j]
        return heads(jax.nn.log_sigmoid(z.astype(f32)) / GLA_GATE_TAU, GLA_HEAD_K)

    def flip(t):
        return t[:, :, ::-1]

    o_f = gla_direction(q, k, v, log_gate(0))
    o_b = flip(gla_direction(flip(q), flip(k), flip(v), flip(log_gate(1))))
    o = o_f + o_b
    o = o * lax.rsqrt(jnp.mean(o * o, axis=-1, keepdims=True) + EPS)
    o = o.transpose(0, 2, 1, 3).reshape(bsz, L, GLA_DV) * norm_g.astype(f32)
    o = o.astype(h.dtype) * jax.nn.silu(r)
    return o @ w_out


def trunk(x, c, ada_w, ada_b, norm1_g, norm2_g,
          s5_lam_re, s5_lam_im, s5_log_dt, s5_b_re, s5_b_im, s5_c_re, s5_c_im, s5_d, s5_w_glu,
          gla_w_in, gla_w_a1, gla_w_a2, gla_b_a, gla_norm_g, gla_w_out,
          mlp_w1, mlp_w2, final_g):
    mod_all = jnp.einsum('bd,ldm->lbm', jax.nn.silu(c), ada_w) + ada_b[:, None]
    for i in range(DEPTH):
        shift1, scale1, gate1, shift2, scale2, gate2 = jnp.split(mod_all[i][:, None, :], N_MOD, axis=-1)
        h = rmsnorm(x, norm1_g[i]) * (1.0 + scale1) + shift1
        j = i // N_MIXERS
        if i % N_MIXERS == 0:
            m = s5_mixer(h, s5_lam_re[j], s5_lam_im[j], s5_log_dt[j], s5_b_re[j], s5_b_im[j],
                         s5_c_re[j], s5_c_im[j], s5_d[j], s5_w_glu[j])
        else:
            m = gla_mixer(h, gla_w_in[j], gla_w_a1[j], gla_w_a2[j], gla_b_a[j], gla_norm_g[j], gla_w_out[j])
        x = x + gate1 * m
        h = rmsnorm(x, norm2_g[i]) * (1.0 + scale2) + shift2
        x = x + gate2 * (jnp.square(jax.nn.relu(h @ mlp_w1[i])) @ mlp_w2[i])
    return rmsnorm(x, final_g)


def setup_inputs(seed: int = 0) -> dict:
    key = jax.random.key(seed)
    ks = iter(jax.random.split(key, 32))
    f32 = jnp.float32

    def nrm(shape, std):
        return std * jax.random.normal(next(ks), shape, f32)

    sa = (N_A_LAYERS, 2, S5_GROUPS, S5_STATE)
    return {
        'x_prompt': nrm((BATCH, SEQ, D_MODEL), 1.0),
        'x_sample': nrm((DEC_BATCH, DEC_SEQ, D_MODEL), 1.0),
        'c_prompt': nrm((BATCH, D_MODEL), 1.0),
        'c_sample': nrm((DEC_BATCH, D_MODEL), 1.0),
        'ada_w': nrm((DEPTH, D_MODEL, N_MOD * D_MODEL), 0.5 * D_MODEL ** -0.5),
        'ada_b': nrm((DEPTH, N_MOD * D_MODEL), 0.02),
        'norm1_g': 1.0 + nrm((DEPTH, D_MODEL), 0.02),
        'norm2_g': 1.0 + nrm((DEPTH, D_MODEL), 0.02),
        's5_lam_re': -0.5 + nrm(sa, 0.01),
        's5_lam_im': math.pi * jnp.arange(S5_STATE, dtype=f32) + nrm(sa, 0.01),
        's5_log_dt': jax.random.uniform(next(ks), (N_A_LAYERS, 2, S5_GROUPS), f32,
                                        math.log(1e-3), math.log(1e-1)),
        's5_b_re': nrm((N_A_LAYERS, 2, S5_GROUPS, S5_STATE, S5_GROUP), (2 * S5_GROUP) ** -0.5),
        's5_b_im': nrm((N_A_LAYERS, 2, S5_GROUPS, S5_STATE, S5_GROUP), (2 * S5_GROUP) ** -0.5),
        's5_c_re': nrm((N_A_LAYERS, 2, S5_GROUPS, S5_GROUP, S5_STATE), 0.7),
        's5_c_im': nrm((N_A_LAYERS, 2, S5_GROUPS, S5_GROUP, S5_STATE), 0.7),
        's5_d': nrm((N_A_LAYERS, D_MODEL), 1.0),
        's5_w_glu': nrm((N_A_LAYERS, D_MODEL, 2 * D_MODEL), D_MODEL ** -0.5),
        'gla_w_in': nrm((N_B_LAYERS, D_MODEL, 2 * GLA_DK + 2 * GLA_DV), D_MODEL ** -0.5),
        'gla_w_a1': nrm((N_B_LAYERS, 2, D_MODEL, GLA_GATE_RANK), D_MODEL ** -0.5),
        'gla_w_a2': nrm((N_B_LAYERS, 2, GLA_GATE_RANK, GLA_DK), GLA_GATE_RANK ** -0.5),
        'gla_b_a': nrm((N_B_LAYERS, 2, GLA_DK), 0.01),
        'gla_norm_g': 1.0 + nrm((N_B_LAYERS, GLA_DV), 0.02),
        'gla_w_out': nrm((N_B_LAYERS, GLA_DV, D_MODEL), GLA_DV ** -0.5),
        'mlp_w1': nrm((DEPTH, D_MODEL, D_FF), D_MODEL ** -0.5),
        'mlp_w2': nrm((DEPTH, D_FF, D_MODEL), D_FF ** -0.5),
        'final_g': 1.0 + nrm((D_MODEL,), 0.02),
    }


def reference(x_prompt, x_sample, c_prompt, c_sample, ada_w, ada_b, norm1_g, norm2_g,
              s5_lam_re, s5_lam_im, s5_log_dt, s5_b_re, s5_b_im, s5_c_re, s5_c_im, s5_d, s5_w_glu,
              gla_w_in, gla_w_a1, gla_w_a2, gla_b_a, gla_norm_g, gla_w_out,
              mlp_w1, mlp_w2, final_g):
    y_prompt = trunk(x_prompt, c_prompt, ada_w, ada_b, norm1_g, norm2_g,
                     s5_lam_re, s5_lam_im, s5_log_dt, s5_b_re, s5_b_im, s5_c_re, s5_c_im, s5_d, s5_w_glu,
                     gla_w_in, gla_w_a1, gla_w_a2, gla_b_a, gla_norm_g, gla_w_out,
                     mlp_w1, mlp_w2, final_g)
    y_sample = trunk(x_sample, c_sample, ada_w, ada_b, norm1_g, norm2_g,
                     s5_lam_re, s5_lam_im, s5_log_dt, s5_b_re, s5_b_im, s5_c_re, s5_c_im, s5_d, s5_w_glu,
                     gla_w_in, gla_w_a1, gla_w_a2, gla_b_a, gla_norm_g, gla_w_out,
                     mlp_w1, mlp_w2, final_g)
    return (y_prompt, y_sample)
```

```python
import math
import numpy as np
import concourse.bass as bass
import concourse.mybir as mybir
from concourse.bass_utils import run_bass_kernel_spmd

F32 = mybir.dt.float32; BF16 = mybir.dt.bfloat16; I32 = mybir.dt.int32
AF = mybir.ActivationFunctionType; ALU = mybir.AluOpType; AX = mybir.AxisListType
ENGS = ("pe", "act", "dve", "pool", "sp")
ENGMAP = {"pe": "tensor", "act": "scalar", "dve": "vector", "pool": "gpsimd", "sp": "sync"}
D = 1024; KT = 8; FF = 4096; EPS = 1e-6


class Buf:
    __slots__ = ("name", "writers", "readers", "war", "dsem", "dcount")

    def __init__(self, name):
        self.name = name; self.writers = []; self.readers = []; self.war = []
        self.dsem = None; self.dcount = 0


class Op:
    __slots__ = ("eng", "fn", "deps", "kind", "sem", "val", "signal", "idx", "waits", "owner", "small")

    def __init__(self, eng, fn, kind):
        self.eng = eng; self.fn = fn; self.kind = kind
        self.deps = []; self.sem = None; self.val = 0; self.signal = False
        self.waits = []; self.owner = None; self.small = False; self.idx = 0


class Sched:
    def __init__(self, nc):
        self.nc = nc
        self.q = {e: [] for e in ENGS}
        self.pending_bar = {e: None for e in ENGS}
        self.dmas_since_bar = []

    def buf(self, name="b"):
        return Buf(name)

    def bufs(self, name, n):
        return [Buf(f"{name}{i}") for i in range(n)]

    def _add(self, op, reads, writes, par):
        deps = []
        for b in reads:
            deps.extend(b.writers)
        for b in writes:
            if b.readers:
                b.war = b.readers; b.readers = []; b.writers = []
            deps.extend(b.war)
            if not par:
                deps.extend(b.writers)
        bar = self.pending_bar[op.eng]
        if bar is not None:
            deps.extend(bar); self.pending_bar[op.eng] = None
        best = {}; dl = []; seen = set()
        for d in deps:
            if d is op:
                continue
            if d.kind == 'c':
                cur = best.get(d.eng)
                if cur is None or cur.idx < d.idx:
                    best[d.eng] = d
            elif id(d) not in seen:
                seen.add(id(d)); dl.append(d)
        op.deps = list(best.values()) + dl
        for b in reads:
            b.readers.append(op)
        for b in writes:
            b.writers.append(op)
        op.idx = len(self.q[op.eng])
        self.q[op.eng].append(op)
        if op.kind != 'c':
            self.dmas_since_bar.append(op)
        return op

    def op(self, eng, fn, reads=(), writes=(), par=False, small=False):
        o = Op(eng, fn, 'c'); o.small = small
        return self._add(o, list(reads), list(writes), par)

    def dma(self, eng, fn, owner, reads=(), writes=(), par=True):
        o = Op(eng, fn, 'd'); o.owner = owner
        return self._add(o, list(reads), list(writes), par)

    def cc(self, fn, reads=(), writes=()):
        return self._add(Op("pool", fn, 'cc'), list(reads), list(writes), False)

    def barrier(self):
        deps = []
        for e in ENGS:
            for o in reversed(self.q[e]):
                if o.kind == 'c':
                    deps.append(o); break
        deps = deps + list(self.dmas_since_bar)
        self.dmas_since_bar = []
        for e in ENGS:
            prev = self.pending_bar[e] or []
            self.pending_bar[e] = prev + deps

    def emit(self, block_ctx, sems):
        sems = list(sems)

        def needs_self_sync(op, d):
            if op.eng == "pool":
                return True
            return d.small and (op.idx - d.idx) <= 2 and op.eng != "pe"

        for e in ENGS:
            for op in self.q[e]:
                for d in op.deps:
                    if d.kind == 'c' and d.eng == op.eng and not needs_self_sync(op, d):
                        continue
                    d.signal = True
        eng_sem = {e: sems.pop() for e in ENGS}
        cc_sem = sems.pop()
        cnt = {e: 0 for e in ENGS}; cccnt = 0
        for e in ENGS:
            for op in self.q[e]:
                if op.kind == 'c':
                    if op.signal:
                        cnt[e] += 1; op.sem = eng_sem[e]; op.val = cnt[e]
                elif op.kind == 'd':
                    b = op.owner
                    if b.dsem is None:
                        b.dsem = sems.pop()
                    b.dcount += 16; op.sem = b.dsem; op.val = b.dcount
                else:
                    cccnt += 1; op.sem = cc_sem; op.val = cccnt
        for e in ENGS:
            known = {}
            for op in self.q[e]:
                need = {}
                for d in op.deps:
                    if d.kind == 'c' and d.eng == e and not needs_self_sync(op, d):
                        continue
                    if d.sem is None:
                        continue
                    k = id(d.sem)
                    if need.get(k, (None, 0))[1] < d.val:
                        need[k] = (d.sem, d.val)
                for k, (s, v) in need.items():
                    if known.get(k, 0) >= v:
                        continue
                    known[k] = v
                    op.waits.append((s, v))
        self.stats = {e: (len(self.q[e]), sum(len(o.waits) for o in self.q[e]), cnt[e]) for e in ENGS}
        self.nsems_left = len(sems)
        dma_final = {}
        for e in ENGS:
            for op in self.q[e]:
                if op.kind in ('d', 'cc'):
                    k = id(op.sem)
                    if dma_final.get(k, (None, 0))[1] < op.val:
                        dma_final[k] = (op.sem, op.val)

        def run(e, eng):
            for op in self.q[e]:
                for (s, v) in op.waits:
                    eng.wait_ge(s, v)
                ins = op.fn(eng)
                if op.kind == 'd':
                    ins.then_inc(op.sem, 16)
                elif op.kind == 'cc':
                    ins.then_inc(op.sem)
                elif op.signal:
                    ins.then_inc(op.sem, 1)
            if e == "sp":
                for (s, v) in dma_final.values():
                    eng.wait_ge(s, v)
                for e2 in ENGS:
                    if cnt[e2] > 0:
                        eng.wait_ge(eng_sem[e2], cnt[e2])

        for e in ENGS:
            getattr(block_ctx, ENGMAP[e])(lambda eng, e=e: run(e, eng))


def _prod(s):
    r = 1
    for v in s:
        r *= v
    return r


def view(ap2d, shape):
    if len(shape) == 1:
        return ap2d
    names = "abcdef"[:len(shape)]
    pat = "p (" + " ".join(names) + ") -> p " + " ".join(names)
    kw = {names[i]: shape[i] for i in range(len(shape))}
    return ap2d.rearrange(pat, **kw)


class Arena:
    def __init__(self, nc, name, nbytes):
        self.nbytes = nbytes
        self.t = nc.alloc_sbuf_tensor(name, [128, nbytes // 2], BF16).ap()
        self.off = 0
        self.hi = 0

    def alloc(self, shape, dtype):
        n = _prod(shape)
        sz = 4 if dtype in (F32, I32) else 2
        nb = (n * sz + 63) // 64 * 64
        assert self.off + nb <= self.nbytes, f"arena overflow {self.off}+{nb}>{self.nbytes}"
        sl = self.t[:, self.off // 2:(self.off + n * sz) // 2]
        if sz == 4:
            sl = sl.bitcast(dtype)
        self.off += nb
        self.hi = max(self.hi, self.off)
        return view(sl, list(shape))

    def reset(self, to=0):
        self.off = to


def bcast_rows(ap1d_tensor, offset, n, parts=128):
    return bass.AP(ap1d_tensor, offset, [[0, parts], [1, n]])


def dram_off(ap):
    return ap.offset


WNAMES = {
    'ada_w': [2, 1024, 6144], 'ada_b': [2, 6144], 'norm1_g': [2, 1024], 'norm2_g': [2, 1024],
    's5_lam_re': [1, 2, 64, 64], 's5_lam_im': [1, 2, 64, 64], 's5_log_dt': [1, 2, 64],
    's5_b_re': [1, 2, 64, 64, 16], 's5_b_im': [1, 2, 64, 64, 16],
    's5_c_re': [1, 2, 64, 16, 64], 's5_c_im': [1, 2, 64, 16, 64],
    's5_d': [1, 1024], 's5_w_glu': [1, 1024, 2048],
    'gla_w_in': [1, 1024, 3072], 'gla_w_a1': [1, 2, 1024, 16], 'gla_w_a2': [1, 2, 16, 512],
    'gla_b_a': [1, 2, 512], 'gla_norm_g': [1, 1024], 'gla_w_out': [1, 1024, 1024],
    'mlp_w1': [2, 1024, 4096], 'mlp_w2': [2, 4096, 1024], 'final_g': [1024],
}
TWO_PI = 2.0 * math.pi


class K:
    pass


def build_program(SEG, NSEG, flags):
    NT = SEG * NSEG
    nc = bass.Bass("TRN2", target_bir_lowering=False)
    k = K(); k.nc = nc; k.SEG = SEG; k.NSEG = NSEG; k.NT = NT; k.flags = flags
    S = Sched(nc); k.S = S
    k.x = nc.dram_tensor("x", [NT, D], F32, kind="ExternalInput").ap()
    k.cvec = nc.dram_tensor("cvec", [NSEG, D], F32, kind="ExternalInput").ap()
    k.keep = nc.dram_tensor("keep", [1, 8], F32, kind="ExternalInput").ap()
    k.w = {n: nc.dram_tensor(n, s, F32, kind="ExternalInput").ap() for n, s in WNAMES.items()}
    k.out = nc.dram_tensor("out", [NT, D], F32, kind="ExternalOutput").ap()
    k.xres = nc.dram_tensor("xres", [NT, D], F32).ap()
    k.modrow = nc.dram_tensor("modrow", [2, NSEG, 6 * D], F32).ap()
    k.B_xres = S.buf("xres"); k.B_modrow = S.buf("modrow"); k.B_out = S.buf("out")
    k.A = Arena(nc, "arena", 200 * 1024)
    k.ps = [nc.alloc_psum_tensor(f"ps{i}", [128, 512], F32).ap() for i in range(6)]
    k.Bps = S.bufs("ps", 6)
    k.psT = [nc.alloc_psum_tensor(f"psT{i}", [128, 1024], BF16).ap() for i in range(2)]
    k.BpsT = S.bufs("psT", 2)
    k.const = Arena(nc, "carena", 6 * 1024)
    k.ident = k.const.alloc([128], BF16)
    k.identf = k.const.alloc([128], F32)
    k.keepb = k.const.alloc([8], F32)
    k.B_ident = S.buf("ident"); k.B_keep = S.buf("keep")
    S.op("pool", lambda e: e.memset(k.identf, 1.0), writes=[k.B_ident])
    S.op("pool", lambda e: e.affine_select(out=k.identf, in_=k.identf, pattern=[[-1, 128]],
                                            compare_op=ALU.is_equal, fill=0.0, base=0, channel_multiplier=1),
         reads=[k.B_ident], writes=[k.B_ident])
    S.op("pool", lambda e: e.tensor_copy(out=k.ident, in_=k.identf), reads=[k.B_ident], writes=[k.B_ident])
    S.dma("sp", lambda e: e.dma_start(out=k.keepb, in_=bass.AP(k.keep.tensor, 0, [[0, 128], [1, 8]])), k.B_keep, writes=[k.B_keep])
    return nc, k


def finish_program(k):
    nc, S = k.nc, k.S
    with nc.Block() as block:
        sems = [nc.alloc_semaphore(f"s{i}") for i in range(100)]
        S.emit(block, sems)
    k.stats = S.stats
    return nc


def phase_mod(k):
    nc, S, A = k.nc, k.S, k.A
    NB = k.NSEG
    cT = A.alloc([NB, 8], F32); sc = A.alloc([NB, 8], F32)
    B_c = S.buf("cT")
    for b in range(NB):
        S.dma("sp", lambda e, b=b: e.dma_start(out=cT[:, b, :], in_=bass.AP(k.cvec.tensor, b * D, [[1, 128], [128, 8]]),
                                              allow_slow_non_contiguous=True), B_c, writes=[B_c])
    S.op("act", lambda e: e.activation(out=sc, in_=cT, func=AF.Silu), reads=[B_c], writes=[B_c])
    modsb = A.alloc([2, 6 * D], F32)
    adab = A.alloc([2, 6 * D], F32)
    g1b = A.alloc([2, D], F32); g2b = A.alloc([2, D], F32)
    B_mod = S.buf("modsb"); B_adab = S.buf("adab"); B_g = S.buf("gb")
    wt = k.w
    S.dma("sp", lambda e: e.dma_start(out=adab[0:NB], in_=bass.AP(wt['ada_b'].tensor, 0, [[0, NB], [1, 2 * 6 * D]])),
          B_adab, writes=[B_adab])
    S.dma("sp", lambda e: e.dma_start(out=g1b[0:NB], in_=bass.AP(wt['norm1_g'].tensor, 0, [[0, NB], [1, 2 * D]])),
          B_g, writes=[B_g])
    S.dma("sp", lambda e: e.dma_start(out=g2b[0:NB], in_=bass.AP(wt['norm2_g'].tensor, 0, [[0, NB], [1, 2 * D]])),
          B_g, writes=[B_g])
    awt = [A.alloc([8, 512], F32) for _ in range(2)]
    B_aw = S.bufs("awt", 2)
    it = 0
    for l in range(2):
        for j in range(12):
            sl = it % 2; it += 1
            src = wt['ada_w'][l, :, 512 * j:512 * (j + 1)].rearrange("(k p) n -> p k n", p=128)
            S.dma("sp", lambda e, sl=sl, src=src: e.dma_start(out=awt[sl], in_=src), B_aw[sl], writes=[B_aw[sl]])
            pb = k.ps[sl]; Bp = k.Bps[sl]
            for kt in range(8):
                S.op("pe", lambda e, sl=sl, kt=kt, pb=pb: e.matmul(pb[0:NB, :], lhsT=sc[:, :, kt], rhs=awt[sl][:, kt, :],
                                                                    start=(kt == 0), stop=(kt == 7)),
                     reads=[B_c, B_aw[sl]], writes=[Bp])
            S.op("dve", lambda e, l=l, j=j, pb=pb: e.tensor_tensor(out=modsb[0:NB, l, 512 * j:512 * (j + 1)], in0=pb[0:NB, :],
                                                                    in1=adab[0:NB, l, 512 * j:512 * (j + 1)], op=ALU.add),
                 reads=[Bp, B_adab], writes=[B_mod], par=True)
    for l in range(2):
        for (c0, gb) in ((1, g1b), (4, g2b)):
            S.op("dve", lambda e, l=l, c0=c0, gb=gb: e.scalar_tensor_tensor(
                out=modsb[0:NB, l, c0 * D:(c0 + 1) * D], in0=modsb[0:NB, l, c0 * D:(c0 + 1) * D], scalar=1.0,
                in1=gb[0:NB, l, :], op0=ALU.add, op1=ALU.mult), reads=[B_mod, B_g], writes=[B_mod])
    S.dma("sp", lambda e: e.dma_start(out=k.modrow.rearrange("l b n -> b l n"), in_=modsb[0:NB]),
          B_mod, reads=[B_mod], writes=[k.B_modrow])


def load_mod_rows(k, l, b, idx, dst, Bdst):
    off = (l * k.NSEG + b) * 6 * D + idx * D
    k.S.dma("sp", lambda e: e.dma_start(out=dst, in_=bass.AP(k.modrow.tensor, off, [[0, 128], [1, D]])),
            Bdst, reads=[k.B_modrow], writes=[Bdst])


def load_mod_cols(k, l, b, idx, dst, Bdst):
    off = (l * k.NSEG + b) * 6 * D + idx * D
    k.S.dma("sp", lambda e: e.dma_start(out=dst, in_=bass.AP(k.modrow.tensor, off, [[1, 128], [128, 8]]),
                                        allow_slow_non_contiguous=True),
            Bdst, reads=[k.B_modrow], writes=[Bdst])


def rms_stats(k, xt_g, ss_col, rstd_col, junk, Bx, Bss, Bjunk):
    S = k.S
    S.op("act", lambda e: e.activation(out=junk, in_=xt_g, func=AF.Square, accum_out=ss_col),
         reads=[Bx], writes=[Bjunk, Bss])
    S.op("dve", lambda e: e.tensor_scalar(out=ss_col, in0=ss_col, scalar1=1.0 / D, scalar2=EPS, op0=ALU.mult, op1=ALU.add),
         reads=[Bss], writes=[Bss], small=True)
    S.op("act", lambda e: e.activation(out=ss_col, in_=ss_col, func=AF.Sqrt), reads=[Bss], writes=[Bss], small=True)
    S.op("dve", lambda e: e.reciprocal(out=rstd_col, in_=ss_col), reads=[Bss], writes=[Bss], small=True)


def load_w_bf16(k, dst, Bdst, src2d, rows_per, ncols):
    R = rows_per
    for r in range(R):
        for c0 in range(0, ncols, 2048):
            c1 = min(ncols, c0 + 2048)
            k.S.dma("pool", lambda e, r=r, c0=c0, c1=c1: e.dma_start(out=dst[:, r, c0:c1], in_=src2d[r * 128:(r + 1) * 128, c0:c1]),
                    Bdst, writes=[Bdst])


def phase_mlp(k, l, src, Bsrc, final):
    nc, S, A = k.nc, k.S, k.A
    TT = 256; NTILE = k.NT // TT
    W1 = A.alloc([8, FF], BF16); W2 = A.alloc([32, D], BF16)
    B_W1 = S.buf("W1"); B_W2 = S.buf("W2")
    load_w_bf16(k, W1, B_W1, k.w['mlp_w1'][l], 8, FF)
    load_w_bf16(k, W2, B_W2, k.w['mlp_w2'][l], 32, D)
    wcol = [A.alloc([8], F32) for _ in range(2)]; shcol = [A.alloc([8], F32) for _ in range(2)]
    grow = [A.alloc([D], F32) for _ in range(2)]
    B_mv = S.bufs("mv", 2)
    B_fg = S.buf("fg")
    if final:
        fgrow = A.alloc([D], F32)
        S.dma("sp", lambda e: e.dma_start(out=fgrow, in_=bass.AP(k.w['final_g'].tensor, 0, [[0, 128], [1, D]])),
              B_fg, writes=[B_fg])
    NB = 2
    xt = [A.alloc([2, D], F32) for _ in range(NB)]; B_xt = S.bufs("xt", NB)
    xn = A.alloc([2, D], BF16); B_xn = S.buf("xn")
    junk = A.alloc([D], BF16); B_junk = S.buf("junk")
    ss = [A.alloc([4], F32) for _ in range(NB)]; B_ss = S.bufs("ss", NB)
    hT = A.alloc([8, TT], BF16); B_hT = S.buf("hT")
    aR = [A.alloc([2, TT], BF16) for _ in range(2)]; B_aR = S.bufs("aR", 2)
    aT = A.alloc([32, TT], BF16); B_aT = S.buf("aT")
    tmp = [A.alloc([512], F32) for _ in range(2)]; B_tmp = S.bufs("tmp", 2)
    if final:
        ot = A.alloc([2, D], F32); B_ot = S.buf("ot")
    psTv = [view(k.psT[i], [4, 256]) for i in range(2)]
    cur_seg = -1
    for t in range(NTILE):
        sl = t % NB
        seg = (t * TT) // k.SEG
        if seg != cur_seg:
            cur_seg = seg; b = seg % 2
            load_mod_cols(k, l, seg, 4, wcol[b], B_mv[b])
            load_mod_cols(k, l, seg, 3, shcol[b], B_mv[b])
            load_mod_rows(k, l, seg, 5, grow[b], B_mv[b])
        rows = src[t * TT:(t + 1) * TT, :].rearrange("(g p) d -> p g d", p=128)
        S.dma("sp", lambda e, sl=sl, rows=rows: e.dma_start(out=xt[sl], in_=rows), B_xt[sl],
              reads=([Bsrc] if Bsrc is not None else []), writes=[B_xt[sl]])
        for g in range(2):
            rms_stats(k, xt[sl][:, g, :], ss[sl][:, g:g + 1], ss[sl][:, 2 + g:3 + g], junk, B_xt[sl], B_ss[sl], B_junk)
            S.op("act", lambda e, sl=sl, g=g: e.activation(out=xn[:, g, :], in_=xt[sl][:, g, :], func=AF.Copy,
                                                           scale=ss[sl][:, 2 + g:3 + g]),
                 reads=[B_xt[sl], B_ss[sl]], writes=[B_xn], par=True)
        for g in range(2):
            for kt in range(8):
                S.op("pe", lambda e, g=g, kt=kt: e.transpose(out=psTv[kt // 4][:, kt % 4, g * 128:(g + 1) * 128],
                                                             in_=xn[:, g, kt * 128:(kt + 1) * 128], identity=k.ident),
                     reads=[B_xn, k.B_ident], writes=[k.BpsT[kt // 4]])
        for kt in range(8):
            if kt // 4 == 0:
                S.op("dve", lambda e, kt=kt, b=b: e.tensor_scalar(out=hT[:, kt, :], in0=psTv[kt // 4][:, kt % 4, :], scalar1=wcol[b][:, kt:kt + 1],
                                                                  scalar2=shcol[b][:, kt:kt + 1], op0=ALU.mult, op1=ALU.add),
                     reads=[k.BpsT[kt // 4], B_mv[b]], writes=[B_hT], par=True)
            else:
                S.op("act", lambda e, kt=kt, b=b: e.activation(out=hT[:, kt, :], in_=psTv[kt // 4][:, kt % 4, :], func=AF.Identity,
                                                               scale=wcol[b][:, kt:kt + 1], bias=shcol[b][:, kt:kt + 1]),
                     reads=[k.BpsT[kt // 4], B_mv[b]], writes=[B_hT], par=True)
        for mp in range(16):
            pb = k.ps[mp % 2]; Bp = k.Bps[mp % 2]
            for mi in range(2):
                m = mp * 2 + mi
                for kt in range(8):
                    S.op("pe", lambda e, pb=pb, mi=mi, m=m, kt=kt: e.matmul(pb[:, mi * 256:(mi + 1) * 256],
                                                                            lhsT=W1[:, kt, m * 128:(m + 1) * 128], rhs=hT[:, kt, :],
                                                                            start=(kt == 0), stop=(kt == 7)),
                         reads=[B_W1, B_hT], writes=[Bp])
            S.op("act", lambda e, pb=pb, mp=mp: e.activation(out=aR[mp % 2], in_=view(pb, [2, 256]), func=AF.Relu),
                 reads=[Bp], writes=[B_aR[mp % 2]])
            S.op("pool", lambda e, mp=mp: e.tensor_tensor(out=aT[:, 2 * mp:2 * mp + 2, :], in0=aR[mp % 2],
                                                          in1=aR[mp % 2], op=ALU.mult),
                 reads=[B_aR[mp % 2]], writes=[B_aT], par=True)
        for g in range(2):
            for h in range(2):
                pi = 2 + (g * 2 + h) % 4
                pb = k.ps[pi]; Bp = k.Bps[pi]
                for m in range(32):
                    S.op("pe", lambda e, pb=pb, m=m, g=g, h=h: e.matmul(pb[:, :], lhsT=aT[:, m, g * 128:(g + 1) * 128],
                                                                        rhs=W2[:, m, h * 512:(h + 1) * 512],
                                                                        start=(m == 0), stop=(m == 31)),
                         reads=[B_aT, B_W2], writes=[Bp])
                ti = (g * 2 + h) % 2
                S.op("dve", lambda e, pb=pb, ti=ti, b=b, h=h: e.tensor_tensor(out=tmp[ti], in0=pb, in1=grow[b][:, h * 512:(h + 1) * 512],
                                                                             op=ALU.mult),
                     reads=[Bp, B_mv[b]], writes=[B_tmp[ti]])
                S.op("pool", lambda e, sl=sl, g=g, h=h, ti=ti: e.tensor_tensor(out=xt[sl][:, g, h * 512:(h + 1) * 512],
                                                                              in0=xt[sl][:, g, h * 512:(h + 1) * 512], in1=tmp[ti], op=ALU.add),
                     reads=[B_tmp[ti], B_xt[sl]], writes=[B_xt[sl]])
        orow = (k.out if final else k.xres)[t * TT:(t + 1) * TT, :].rearrange("(g p) d -> p g d", p=128)
        if not final:
            S.dma("sp", lambda e, sl=sl, orow=orow: e.dma_start(out=orow, in_=xt[sl]), B_xt[sl], reads=[B_xt[sl]], writes=[k.B_xres])
        else:
            for g in range(2):
                rms_stats(k, xt[sl][:, g, :], ss[sl][:, g:g + 1], ss[sl][:, 2 + g:3 + g], junk, B_xt[sl], B_ss[sl], B_junk)
                S.op("dve", lambda e, sl=sl, g=g: e.scalar_tensor_tensor(out=ot[:, g, :], in0=xt[sl][:, g, :],
                                                                          scalar=ss[sl][:, 2 + g:3 + g], in1=fgrow,
                                                                          op0=ALU.mult, op1=ALU.mult),
                     reads=[B_xt[sl], B_ss[sl], B_fg], writes=[B_ot], par=True)
            S.dma("sp", lambda e, orow=orow: e.dma_start(out=orow, in_=ot), B_ot, reads=[B_ot], writes=[k.B_out])


def AP_(t, off, dims):
    return bass.AP(t.tensor, t.offset + off, [list(t.ap[0])] + [list(d) for d in dims])


def s5_prep(k):
    nc, S, A = k.nc, k.S, k.A
    w = k.w
    kw = dict(kind="ExternalOutput") if k.flags.get("dbg_s5mats") else {}
    k.sG = nc.dram_tensor("sG", [2, 128, 64 * 128], BF16, **kw).ap()
    k.sD0 = nc.dram_tensor("sD0", [2, 128, 64 * 128], BF16, **kw).ap()
    k.sCr = nc.dram_tensor("sCr", [2, 128, 2 * 32 * 256], BF16, **kw).ap()
    k.B_smat = S.buf("smat")
    k.c1 = k.const.alloc([2, 2, 32], F32); k.c2 = k.const.alloc([2, 2, 32], F32)
    k.B_coef = S.buf("coef")
    B = S.buf("prep")
    Bp0, Bp1 = k.Bps[0], k.Bps[1]

    def dve(fn, reads=(), writes=(), small=True):
        S.op("dve", fn, reads=[B] + list(reads), writes=[B] + list(writes), small=small)

    def act(fn, reads=(), writes=()):
        S.op("act", fn, reads=[B] + list(reads), writes=[B] + list(writes), small=True)

    sgnY = A.alloc([1], F32)
    sgnB = A.alloc([1], F32)
    neg1 = A.alloc([1], F32)
    S.op("pool", lambda e: e.memset(sgnY[0:64], 1.0), writes=[B])
    S.op("pool", lambda e: e.memset(sgnY[64:128], -1.0), writes=[B])
    S.op("pool", lambda e: e.memset(sgnB[0:64], -1.0), writes=[B])
    S.op("pool", lambda e: e.memset(sgnB[64:128], 1.0), writes=[B])
    S.op("pool", lambda e: e.memset(neg1, -1.0), writes=[B])
    evi = A.alloc([16], I32); ev = A.alloc([16], F32)
    S.op("pool", lambda e: e.iota(evi, pattern=[[1, 16]], base=-7, channel_multiplier=0), writes=[B])
    dve(lambda e: e.tensor_copy(out=ev, in_=evi))
    ones = A.alloc([128], F32); maskf = A.alloc([128], F32); maskb = A.alloc([128], F32)
    S.op("pool", lambda e: e.memset(ones, 1.0), writes=[B])
    S.op("pool", lambda e: e.affine_select(out=view(maskf, [8, 16]), in_=view(ones, [8, 16]), pattern=[[16, 8], [0, 16]],
                                            compare_op=ALU.is_ge, fill=0.0, base=15, channel_multiplier=-1), reads=[B], writes=[B])
    S.op("pool", lambda e: e.affine_select(out=view(maskb, [8, 16]), in_=view(ones, [8, 16]), pattern=[[-16, 8], [0, 16]],
                                            compare_op=ALU.is_ge, fill=0.0, base=0, channel_multiplier=1), reads=[B], writes=[B])
    base_off = A.off
    for d_ in range(2):
        A.reset(base_off)
        _prep_dir(k, d_, B, dve, act, sgnY, sgnB, ev, maskf, maskb)


def _prep_dir(k, d, B, dve, act, sgnY, sgnB, ev, maskf, maskb):
    nc, S, A = k.nc, k.S, k.A
    w = k.w
    if True:
        fwd = (d == 0)
        lamr = A.alloc([64], F32); lami = A.alloc([64], F32); ldt = A.alloc([64], F32)
        for h in range(2):
            S.dma("sp", lambda e, h=h, d=d: e.dma_start(out=lamr[h * 64:(h + 1) * 64],
                                                        in_=bass.AP(w['s5_lam_re'].tensor, d * 4096, [[1, 64], [64, 64]]), allow_slow_non_contiguous=True),
                  B, writes=[B])
            S.dma("sp", lambda e, h=h, d=d: e.dma_start(out=lami[h * 64:(h + 1) * 64],
                                                        in_=bass.AP(w['s5_lam_im'].tensor, d * 4096, [[1, 64], [64, 64]]), allow_slow_non_contiguous=True),
                  B, writes=[B])
        S.dma("sp", lambda e, d=d: e.dma_start(out=ldt, in_=bass.AP(w['s5_log_dt'].tensor, d * 64, [[0, 128], [1, 64]])), B, writes=[B])
        dtv = A.alloc([64], F32); lrdt = A.alloc([64], F32); lidt = A.alloc([64], F32)
        act(lambda e: e.activation(out=dtv, in_=ldt, func=AF.Exp))
        dve(lambda e: e.tensor_tensor(out=lrdt, in0=lamr, in1=dtv, op=ALU.mult))
        dve(lambda e: e.tensor_tensor(out=lidt, in0=lami, in1=dtv, op=ALU.mult))
        def bc_g(ap64):
            return AP_(ap64, 0, [[1, 64], [0, 16]])
        def bc_e(ap16):
            return AP_(ap16, 0, [[0, 64], [1, 16]])
        argr = A.alloc([64, 16], F32); argi = A.alloc([64, 16], F32)
        dve(lambda e: e.tensor_tensor(out=argr, in0=bc_g(lrdt), in1=bc_e(ev), op=ALU.mult))
        dve(lambda e: e.tensor_tensor(out=argi, in0=bc_g(lidt), in1=bc_e(ev), op=ALU.mult))
        mag = A.alloc([64, 16], F32)
        act(lambda e: e.activation(out=mag, in_=argr, func=AF.Exp))
        t1 = A.alloc([64, 16], F32); ki = A.alloc([64, 16], I32); kf = A.alloc([64, 16], F32)
        sn = A.alloc([64, 16], F32); cs = A.alloc([64, 16], F32)

        def sin_of(dst, const):
            dve(lambda e: e.tensor_scalar(out=t1, in0=argi, scalar1=1.0 / TWO_PI, scalar2=const / TWO_PI, op0=ALU.mult, op1=ALU.add))
            dve(lambda e: e.tensor_copy(out=ki, in_=t1))
            dve(lambda e: e.tensor_copy(out=kf, in_=ki))
            dve(lambda e: e.tensor_scalar(out=t1, in0=argi, scalar1=const, scalar2=None, op0=ALU.add))
            dve(lambda e: e.scalar_tensor_tensor(out=t1, in0=kf, scalar=-TWO_PI, in1=t1, op0=ALU.mult, op1=ALU.add))
            dve(lambda e: e.tensor_scalar(out=t1, in0=t1, scalar1=math.pi, scalar2=-math.pi, op0=ALU.min, op1=ALU.max))
            act(lambda e: e.activation(out=dst, in_=t1, func=AF.Sin))
        sin_of(sn, 0.0)
        sin_of(cs, math.pi / 2)
        Ar = A.alloc([64, 16], F32); Ai = A.alloc([64, 16], F32)
        dve(lambda e: e.tensor_tensor(out=Ar, in0=mag, in1=cs, op=ALU.mult))
        dve(lambda e: e.tensor_tensor(out=Ai, in0=mag, in1=sn, op=ALU.mult))
        abr = Ar[:, :, 8]; abi = Ai[:, :, 8]
        den = A.alloc([64], F32); u1 = A.alloc([64], F32); u2 = A.alloc([64], F32); am1 = A.alloc([64], F32)
        zr = A.alloc([64], F32); zi = A.alloc([64], F32)
        dve(lambda e: e.tensor_tensor(out=den, in0=lamr, in1=lamr, op=ALU.mult))
        dve(lambda e: e.tensor_tensor(out=u1, in0=lami, in1=lami, op=ALU.mult))
        dve(lambda e: e.tensor_tensor(out=den, in0=den, in1=u1, op=ALU.add))
        dve(lambda e: e.reciprocal(out=den, in_=den))
        dve(lambda e: e.tensor_scalar(out=am1, in0=abr, scalar1=-1.0, scalar2=None, op0=ALU.add))
        dve(lambda e: e.tensor_tensor(out=u1, in0=am1, in1=lamr, op=ALU.mult))
        dve(lambda e: e.tensor_tensor(out=u2, in0=abi, in1=lami, op=ALU.mult))
        dve(lambda e: e.tensor_tensor(out=u1, in0=u1, in1=u2, op=ALU.add))
        dve(lambda e: e.tensor_tensor(out=zr, in0=u1, in1=den, op=ALU.mult))
        dve(lambda e: e.tensor_tensor(out=u1, in0=abi, in1=lamr, op=ALU.mult))
        dve(lambda e: e.tensor_tensor(out=u2, in0=am1, in1=lami, op=ALU.mult))
        dve(lambda e: e.tensor_tensor(out=u1, in0=u1, in1=u2, op=ALU.subtract))
        dve(lambda e: e.tensor_tensor(out=zi, in0=u1, in1=den, op=ALU.mult))
        Wr = argr; Wi = argi; w1 = mag
        dve(lambda e: e.tensor_tensor(out=Wr, in0=Ar, in1=bc_g(zr), op=ALU.mult))
        dve(lambda e: e.tensor_tensor(out=w1, in0=Ai, in1=bc_g(zi), op=ALU.mult))
        dve(lambda e: e.tensor_tensor(out=Wr, in0=Wr, in1=w1, op=ALU.subtract))
        dve(lambda e: e.tensor_tensor(out=Wi, in0=Ar, in1=bc_g(zi), op=ALU.mult))
        dve(lambda e: e.tensor_tensor(out=w1, in0=Ai, in1=bc_g(zr), op=ALU.mult))
        dve(lambda e: e.tensor_tensor(out=Wi, in0=Wi, in1=w1, op=ALU.add))
        for gh in range(2):
            ps_ = slice(gh * 64, (gh + 1) * 64)
            for c in range(2):
                S.op("dve", lambda e, ps_=ps_, gh=gh, c=c, d=d: e.tensor_copy(out=k.c1[ps_, d, c, :], in_=Ar[ps_, gh * 32:(gh + 1) * 32, 15]),
                     reads=[B], writes=[k.B_coef])
            S.op("dve", lambda e, ps_=ps_, gh=gh, d=d: e.tensor_scalar(out=k.c2[ps_, d, 0, :], in0=Ai[ps_, gh * 32:(gh + 1) * 32, 15], scalar1=-1.0,
                                                                      scalar2=None, op0=ALU.mult), reads=[B], writes=[k.B_coef])
            S.op("dve", lambda e, ps_=ps_, gh=gh, d=d: e.tensor_copy(out=k.c2[ps_, d, 1, :], in_=Ai[ps_, gh * 32:(gh + 1) * 32, 15]),
                 reads=[B], writes=[k.B_coef])
        Bx = t1; Btx = kf
        def bsrc(name, d):
            return bass.AP(w[name].tensor, d * 65536, [[16, 64], [1024, 64], [1, 16]])
        S.dma("sp", lambda e, d=d: e.dma_start(out=Bx[0:64], in_=bsrc('s5_b_re', d)), B, writes=[B])
        S.dma("sp", lambda e, d=d: e.dma_start(out=Bx[64:128], in_=bsrc('s5_b_im', d)), B, writes=[B])
        S.dma("sp", lambda e, d=d: e.dma_start(out=Btx[0:64], in_=bsrc('s5_b_im', d)), B, writes=[B])
        S.dma("sp", lambda e, d=d: e.dma_start(out=Btx[64:128], in_=bsrc('s5_b_re', d)), B, writes=[B])
        dve(lambda e: e.tensor_scalar(out=Btx, in0=Btx, scalar1=sgnB[:, 0:1], scalar2=None, op0=ALU.mult))
        Cx = sn; Ctx = cs
        CreX = A.alloc([32, 16], F32); CimX = A.alloc([32, 16], F32)
        cst = A.alloc([8, 128], F32)
        def csrc(name, d, g0, ng):
            return bass.AP(w[name].tensor, d * 65536 + g0 * 1024, [[64, ng * 16], [1, 64]])
        def load_Y(first, second):
            for tI in range(8):
                S.dma("sp", lambda e, tI=tI: e.dma_start(out=cst[:, tI, 0:64], in_=csrc(first, d, tI * 8, 8)), B, writes=[B])
                S.dma("sp", lambda e, tI=tI: e.dma_start(out=cst[:, tI, 64:128], in_=csrc(second, d, tI * 8, 8)), B, writes=[B])
        def transposes_to(dst_fn, ntile, evac):
            for tI in range(ntile):
                pb = k.ps[tI % 2]; Bp = k.Bps[tI % 2]
                S.op("pe", lambda e, tI=tI, pb=pb: e.transpose(out=pb[:, 0:128], in_=cst[:, tI, :], identity=k.identf),
                     reads=[B, k.B_ident], writes=[Bp])
                S.op("dve", lambda e, tI=tI, pb=pb: evac(e, dst_fn(tI), pb[:, 0:128]), reads=[Bp, B], writes=[B])
        load_Y('s5_c_re', 's5_c_im')
        transposes_to(lambda tI: Cx[:, tI * 8:(tI + 1) * 8, :], 8,
                      lambda e, dst, src: e.tensor_scalar(out=dst, in0=view(src, [8, 16]), scalar1=sgnY[:, 0:1], scalar2=None, op0=ALU.mult))
        load_Y('s5_c_im', 's5_c_re')
        transposes_to(lambda tI: Ctx[:, tI * 8:(tI + 1) * 8, :], 8,
                      lambda e, dst, src: e.tensor_scalar(out=dst, in0=view(src, [8, 16]), scalar1=-1.0, scalar2=None, op0=ALU.mult))
        for (name, dstX) in (('s5_c_re', CreX), ('s5_c_im', CimX)):
            for tI in range(4):
                S.dma("sp", lambda e, tI=tI, name=name: e.dma_start(out=cst[:, tI, 0:64], in_=csrc(name, d, tI * 8, 8)), B, writes=[B])
                S.dma("sp", lambda e, tI=tI, name=name: e.dma_start(out=cst[:, tI, 64:128], in_=csrc(name, d, 32 + tI * 8, 8)), B, writes=[B])
            transposes_to(lambda tI, dstX=dstX: dstX[:, tI * 8:(tI + 1) * 8, :], 4,
                          lambda e, dst, src: e.tensor_copy(out=dst, in_=view(src, [8, 16])))
        def eslice(ap3, lo_e, step):
            return AP_(ap3, lo_e + 7, [[16, ap3.shape[1]], [step, 8]])
        if fwd:
            e2 = (0, -1); eG = (7, -1); eD = (0, 1); eC = (1, 1)
        else:
            e2 = (-7, 1); eG = (0, 1); eD = (7, -1); eC = (8, -1)
        tA = A.alloc([32, 8, 16], F32); tB = A.alloc([32, 8, 16], F32)
        big0 = A.off
        E2 = A.alloc([64, 8, 16], BF16); E3 = A.alloc([64, 8, 16], BF16); E1D = A.alloc([64, 8, 16], BF16)
        def bc3(ap_ge, lo_e, step, G=64):
            return AP_(ap_ge, lo_e + 7, [[16, G], [step, 8], [0, 16]])
        def bc_mid(ap_gi, G=64):
            return AP_(ap_gi, 0, [[16, G], [0, 8], [1, 16]])
        def make_E(dst, cr, ci, ex, X1, X2):
            for hf in range(2):
                gs = slice(hf * 32, (hf + 1) * 32)
                dve(lambda e, gs=gs: e.tensor_tensor(out=tA, in0=bc3(cr[:, gs], *ex, G=32), in1=bc_mid(X1[:, gs], G=32), op=ALU.mult))
                S.op("pool", lambda e, gs=gs: e.tensor_tensor(out=tB, in0=bc3(ci[:, gs], *ex, G=32), in1=bc_mid(X2[:, gs], G=32), op=ALU.mult), reads=[B], writes=[B])
                dve(lambda e, gs=gs: e.tensor_tensor(out=dst[:, gs], in0=tA, in1=tB, op=ALU.add))
        make_E(E2, Wr, Wi, e2, Bx, Btx)
        make_E(E3, Wr, Wi, eG, Bx, Btx)
        make_E(E1D, Ar, Ai, eD, Cx, Ctx)
        D0 = A.alloc([64, 128], BF16); Gm = A.alloc([64, 128], BF16)
        mk = maskf if fwd else maskb
        for g in range(64):
            pb = k.ps[g % 2]; Bp = k.Bps[g % 2]
            S.op("pe", lambda e, g=g, pb=pb: e.matmul(pb[:, 0:128], lhsT=E2[:, g, :, :].rearrange("p a b -> p (a b)"),
                                                      rhs=E1D[:, g, :, :].rearrange("p a b -> p (a b)"), start=True, stop=True),
                 reads=[B], writes=[Bp])
            S.op("dve", lambda e, g=g, pb=pb: e.tensor_tensor(out=D0[:, g, :], in0=pb[:, 0:128], in1=mk, op=ALU.mult),
                 reads=[Bp, B], writes=[B])
        psTv = [view(k.psT[i], [8, 128]) for i in range(2)]
        for gb in range(8):
            pt = psTv[gb % 2]; Bpt = k.BpsT[gb % 2]
            for gi in range(8):
                g = gb * 8 + gi
                S.op("pe", lambda e, g=g, gi=gi, pt=pt: e.transpose(out=pt[:, gi, :], in_=E3[:, g, :, :].rearrange("p a b -> p (a b)"),
                                                                    identity=k.ident), reads=[B, k.B_ident], writes=[Bpt])
            eng = "dve" if gb % 2 == 0 else "act"
            if eng == "dve":
                S.op("dve", lambda e, gb=gb, pt=pt: e.tensor_copy(out=Gm[:, gb * 8:(gb + 1) * 8, :], in_=pt), reads=[Bpt, B], writes=[B])
            else:
                S.op("act", lambda e, gb=gb, pt=pt: e.activation(out=Gm[:, gb * 8:(gb + 1) * 8, :], in_=pt, func=AF.Copy), reads=[Bpt, B], writes=[B])
        S.dma("sp", lambda e, d=d: e.dma_start(out=k.sD0[d], in_=D0.rearrange("p a b -> p (a b)")), B, reads=[B], writes=[k.B_smat])
        S.dma("sp", lambda e, d=d: e.dma_start(out=k.sG[d], in_=Gm.rearrange("p a b -> p (a b)")), B, reads=[B], writes=[k.B_smat])
        ArX = ki.bitcast(F32)[:, 0:32, :]; AiX = ki.bitcast(F32)[:, 32:64, :]
        for gh in range(2):
            ps_ = slice(gh * 64, (gh + 1) * 64)
            dve(lambda e, ps_=ps_, gh=gh: e.tensor_copy(out=ArX[ps_], in_=Ar[ps_, gh * 32:(gh + 1) * 32, :]))
            dve(lambda e, ps_=ps_, gh=gh: e.tensor_copy(out=AiX[ps_], in_=Ai[ps_, gh * 32:(gh + 1) * 32, :]))
        xa = tA; xb = tB
        A.reset(big0)
        ECr = A.alloc([2, 32, 8, 16], BF16)
        dve(lambda e: e.tensor_tensor(out=xa, in0=bc3(ArX, *eC, G=32), in1=bc_mid(CreX, G=32), op=ALU.mult))
        S.op("pool", lambda e: e.tensor_tensor(out=xb, in0=bc3(AiX, *eC, G=32), in1=bc_mid(CimX, G=32), op=ALU.mult), reads=[B], writes=[B])
        dve(lambda e: e.tensor_tensor(out=ECr[:, 0], in0=xa, in1=xb, op=ALU.subtract))
        dve(lambda e: e.tensor_tensor(out=xa, in0=bc3(ArX, *eC, G=32), in1=bc_mid(CimX, G=32), op=ALU.mult))
        S.op("pool", lambda e: e.tensor_tensor(out=xb, in0=bc3(AiX, *eC, G=32), in1=bc_mid(CreX, G=32), op=ALU.mult), reads=[B], writes=[B])
        dve(lambda e: e.tensor_tensor(out=xa, in0=xa, in1=xb, op=ALU.add))
        dve(lambda e: e.tensor_scalar(out=ECr[:, 1], in0=xa, scalar1=-1.0, scalar2=None, op0=ALU.mult))
        CrP = A.alloc([2, 32, 2, 128], BF16)
        S.op("pool", lambda e: e.memset(CrP, 0.0), reads=[B], writes=[B])
        for gh in range(2):
            ps_ = slice(gh * 64, (gh + 1) * 64)
            for c in range(2):
                dve(lambda e, ps_=ps_, gh=gh, c=c: e.tensor_copy(out=CrP[ps_, c, :, gh, :], in_=ECr[ps_, c].rearrange("p g a b -> p g (a b)")))
        S.dma("sp", lambda e, d=d: e.dma_start(out=k.sCr[d], in_=CrP.rearrange("p a b c d -> p (a b c d)")), B, reads=[B], writes=[k.B_smat])


def phase_s5(k):
    nc, S, A = k.nc, k.S, k.A
    NT = k.NT; NTILE = NT // 1024; TPS = k.SEG // 1024
    kw = dict(kind="ExternalOutput") if k.flags.get("dbg_s5") else {}
    k.hperm = nc.dram_tensor("hperm", [NT // 8, 64 * 128], BF16, **kw).ap()
    k.yf = nc.dram_tensor("yfs", [NT // 8, 64 * 128], BF16, **kw).ap()
    k.gout = nc.dram_tensor("gout", [NT, D], BF16, **kw).ap()
    B_hperm = S.buf("hperm"); B_yf = S.buf("yf"); k.B_gout = S.buf("gout")
    s5_prep(k)
    S.barrier(); A.reset()
    w1row = [A.alloc([D], F32) for _ in range(2)]; sh1row = [A.alloc([D], F32) for _ in range(2)]
    B_mv = S.bufs("mv1", 2)
    xs = [A.alloc([D], F32) for _ in range(3)]; B_xs = S.bufs("xs", 3)
    xw = [A.alloc([D], F32) for _ in range(2)]; B_xw = S.bufs("xw", 2)
    junk = A.alloc([D], BF16); B_junk = S.buf("junk")
    ss = [A.alloc([16], F32) for _ in range(2)]; B_ss = S.bufs("ss", 2)
    hp = [A.alloc([64, 8, 16], BF16) for _ in range(2)]; B_hp = S.bufs("hp", 2)
    cur_seg = -1; it = 0
    for t in range(NTILE):
        seg = t // TPS
        if seg != cur_seg:
            cur_seg = seg; b = seg % 2
            load_mod_rows(k, 0, seg, 1, w1row[b], B_mv[b])
            load_mod_rows(k, 0, seg, 0, sh1row[b], B_mv[b])
        hs = t % 2
        for s in range(8):
            xi = it % 3; wi = it % 2; it += 1
            src = bass.AP(k.x.tensor, (t * 1024 + s) * D, [[8 * D, 128], [1, D]])
            S.dma("sp", lambda e, xi=xi, src=src: e.dma_start(out=xs[xi], in_=src), B_xs[xi], writes=[B_xs[xi]])
            rms_stats(k, xs[xi], ss[hs][:, s:s + 1], ss[hs][:, 8 + s:9 + s], junk, B_xs[xi], B_ss[hs], B_junk)
            S.op("pool", lambda e, xi=xi, wi=wi, b=b: e.tensor_tensor(out=xw[wi], in0=xs[xi], in1=w1row[b], op=ALU.mult),
                 reads=[B_xs[xi], B_mv[b]], writes=[B_xw[wi]])
            S.op("dve", lambda e, wi=wi, hs=hs, s=s, b=b: e.scalar_tensor_tensor(
                out=hp[hs][:, :, s, :], in0=view(xw[wi], [64, 16]), scalar=ss[hs][:, 8 + s:9 + s], in1=view(sh1row[b], [64, 16]),
                op0=ALU.mult, op1=ALU.add), reads=[B_xw[wi], B_ss[hs], B_mv[b]], writes=[B_hp[hs]], par=True)
        S.dma("sp", lambda e, t=t, hs=hs: e.dma_start(out=k.hperm[t * 128:(t + 1) * 128, :], in_=hp[hs].rearrange("p a b c -> p (a b c)")),
              B_hp[hs], reads=[B_hp[hs]], writes=[B_hperm])
    S.barrier(); A.reset()
    Gm = A.alloc([64, 128], BF16); D0 = A.alloc([64, 128], BF16); CrP = A.alloc([2, 32, 256], BF16)
    B_mat = S.buf("mats")
    hpt = A.alloc([64, 128], BF16); B_hpt = S.buf("hpt")
    U = A.alloc([64, 128], BF16); B_U = S.buf("U")
    Hloc = A.alloc([2, 32, 128], BF16); B_Hloc = S.buf("Hloc")
    Hb = A.alloc([2, 32, 130], BF16); B_Hb = S.buf("Hb")
    st = [A.alloc([2, 32], F32) for _ in range(3)]; B_st = S.bufs("st", 3)
    m1 = A.alloc([2, 32], F32); m2 = A.alloc([2, 32], F32); tt = A.alloc([2, 32], F32)
    B_m1 = S.buf("m1"); B_m2 = S.buf("m2"); B_tt = S.buf("tt")
    fill = A.alloc([8], F32); B_fill = S.buf("fill")
    yft = A.alloc([64, 128], BF16); B_yft = S.buf("yft")
    gt = A.alloc([8, D], BF16); B_gt = S.buf("gt")
    tmpy = [A.alloc([2, 2, 128], F32) for _ in range(2)]; B_tmpy = S.bufs("tmpy", 2)
    hd = A.alloc([64, 128], BF16); B_hd = S.buf("hd")
    drow = A.alloc([D], F32); B_drow = S.buf("drow")
    S.dma("sp", lambda e: e.dma_start(out=drow, in_=bass.AP(k.w['s5_d'].tensor, 0, [[0, 128], [1, D]])), B_drow, writes=[B_drow])
    S.op("pool", lambda e: e.memset(fill, 0.0), writes=[B_fill])
    psTv = [view(k.psT[i], [8, 128]) for i in range(2)]

    def swp(ap):
        return bass.AP(ap.tensor, ap.offset + 32, [list(ap.ap[0]), [-32, 2], [1, 32]])

    for d in range(2):
        fwd = (d == 0)
        S.dma("sp", lambda e, d=d: e.dma_start(out=Gm.rearrange("p a b -> p (a b)"), in_=k.sG[d]), B_mat, reads=[k.B_smat], writes=[B_mat])
        S.dma("sp", lambda e, d=d: e.dma_start(out=D0.rearrange("p a b -> p (a b)"), in_=k.sD0[d]), B_mat, reads=[k.B_smat], writes=[B_mat])
        S.dma("sp", lambda e, d=d: e.dma_start(out=CrP.rearrange("p a b c -> p (a b c)"), in_=k.sCr[d]), B_mat, reads=[k.B_smat], writes=[B_mat])
        S.op("dve", lambda e: e.memset(st[0], 0.0), writes=[B_st[0]], small=True)
        S.op("pool", lambda e: e.memset(Hb, 0.0), writes=[B_Hb])
        sidx = 0
        tiles = list(range(NTILE)) if fwd else list(range(NTILE - 1, -1, -1))
        for ti, t in enumerate(tiles):
            seg = t // TPS
            S.dma("sp", lambda e, t=t: e.dma_start(out=hpt.rearrange("p a b -> p (a b)"), in_=k.hperm[t * 128:(t + 1) * 128, :]),
                  B_hpt, reads=[B_hperm], writes=[B_hpt])
            for gb in range(8):
                pt = psTv[gb % 2]; Bpt = k.BpsT[gb % 2]
                for gi in range(8):
                    g = gb * 8 + gi
                    S.op("pe", lambda e, g=g, gi=gi, pt=pt: e.transpose(out=pt[:, gi, :], in_=hpt[:, g, :], identity=k.ident),
                         reads=[B_hpt, k.B_ident], writes=[Bpt])
                S.op("act", lambda e, gb=gb, pt=pt: e.activation(out=U[:, gb * 8:(gb + 1) * 8, :], in_=pt, func=AF.Copy),
                     reads=[Bpt], writes=[B_U], par=True)
            for q in range(8):
                banks = (0, 1) if q % 2 == 0 else (2, 3)
                for c in range(2):
                    pb = k.ps[banks[c]]; Bp = k.Bps[banks[c]]
                    pv = view(pb, [4, 128])
                    for j in range(4):
                        gl = q * 4 + j
                        for gh in range(2):
                            g = gh * 32 + gl
                            S.op("pe", lambda e, pv=pv, j=j, gh=gh, g=g, c=c: e.matmul(pv[gh * 64:(gh + 1) * 64, j, :],
                                                                                       lhsT=Gm[:, g, c * 64:(c + 1) * 64], rhs=U[:, g, :],
                                                                                       start=True, stop=True),
                                 reads=[B_mat, B_U], writes=[Bp])
                    S.op("act", lambda e, pv=pv, c=c, q=q: e.activation(out=Hloc[:, c, q * 4:(q + 1) * 4, :], in_=pv, func=AF.Copy),
                         reads=[Bp], writes=[B_Hloc], par=True)
            at_seg_start = (t % TPS == 0) if fwd else (t % TPS == TPS - 1)
            ccol = 0 if fwd else 128
            pcol = 128 if fwd else 0
            kidx = seg if fwd else seg + 1
            if ti > 0:
                if at_seg_start:
                    S.op("act", lambda e, ccol=ccol, pcol=pcol, kidx=kidx: e.activation(out=Hb[:, :, :, ccol], in_=Hb[:, :, :, pcol], func=AF.Copy,
                                                                                     scale=k.keepb[:, kidx:kidx + 1]),
                         reads=[B_Hb, k.B_keep], writes=[B_Hb])
                    pbuf = sidx % 3
                    S.op("dve", lambda e, pbuf=pbuf, kidx=kidx: e.tensor_scalar(out=st[pbuf], in0=st[pbuf], scalar1=k.keepb[:, kidx:kidx + 1],
                                                                              scalar2=None, op0=ALU.mult),
                         reads=[B_st[pbuf], k.B_keep], writes=[B_st[pbuf]], small=True)
                else:
                    S.op("act", lambda e, ccol=ccol, pcol=pcol: e.activation(out=Hb[:, :, :, ccol], in_=Hb[:, :, :, pcol], func=AF.Copy),
                         reads=[B_Hb], writes=[B_Hb])
            order = range(128) if fwd else range(127, -1, -1)
            import os
            SM = bool(int(os.environ.get("CHAIN_SYNC", "0")))
            for n in order:
                pv_ = sidx % 3; nx = (sidx + 1) % 3; sidx += 1
                S.op("dve", lambda e, pv_=pv_, d=d: e.tensor_tensor(out=m1, in0=st[pv_], in1=k.c1[:, d], op=ALU.mult),
                     reads=[B_st[pv_], k.B_coef], writes=[B_m1], small=SM)
                S.op("dve", lambda e, pv_=pv_, d=d: e.tensor_tensor(out=m2, in0=swp(st[pv_]), in1=k.c2[:, d], op=ALU.mult),
                     reads=[B_st[pv_], k.B_coef], writes=[B_m2], small=SM)
                S.op("dve", lambda e, n=n: e.tensor_tensor(out=tt, in0=m1, in1=Hloc[:, :, :, n], op=ALU.add),
                     reads=[B_m1, B_Hloc], writes=[B_tt], small=SM)
                S.op("dve", lambda e: e.tensor_copy(out=fill[:, 0:2], in_=fill[:, 2:4]), reads=[], writes=[])
                S.op("dve", lambda e, nx=nx: e.tensor_tensor(out=st[nx], in0=tt, in1=m2, op=ALU.add),
                     reads=[B_tt, B_m2], writes=[B_st[nx]], small=SM)
                S.op("dve", lambda e: e.tensor_copy(out=fill[:, 4:6], in_=fill[:, 6:8]), reads=[], writes=[])
                col = n + 1 if fwd else n
                S.op("act", lambda e, nx=nx, col=col: e.activation(out=Hb[:, :, :, col], in_=st[nx], func=AF.Copy),
                     reads=[B_st[nx]], writes=[B_Hb])
            if not fwd:
                S.dma("sp", lambda e, t=t: e.dma_start(out=yft.rearrange("p a b -> p (a b)"), in_=k.yf[t * 128:(t + 1) * 128, :]),
                      B_yft, reads=[B_yf], writes=[B_yft])
                S.op("pool", lambda e: e.tensor_tensor(out=hd.rearrange("p g (t j) -> p g t j", t=8),
                                                       in0=hpt.rearrange("p g (t j) -> p g t j", t=8),
                                                       in1=bass.AP(drow.tensor, drow.offset, [list(drow.ap[0]), [16, 64], [0, 8], [1, 16]]), op=ALU.mult),
                     reads=[B_hpt, B_drow], writes=[B_hd])
            c0 = 0 if fwd else 1
            for q in range(16):
                pb = k.ps[4 + q % 2]; Bp = k.Bps[4 + q % 2]
                pv = view(pb, [2, 256])
                for j in range(2):
                    gl = q * 2 + j
                    for c in range(2):
                        S.op("pe", lambda e, pv=pv, j=j, gl=gl, c=c, c0=c0: e.matmul(pv[:, j, :], lhsT=Hb[:, c, gl, c0:c0 + 128], rhs=CrP[:, c, gl, :],
                                                                              start=(c == 0), stop=False),
                             reads=[B_Hb, B_mat], writes=[Bp])
                    for gh in range(2):
                        g = gh * 32 + gl
                        S.op("pe", lambda e, pv=pv, j=j, gh=gh, g=g: e.matmul(pv[:, j, gh * 128:(gh + 1) * 128], lhsT=U[:, g, :], rhs=D0[:, g, :],
                                                                              start=False, stop=(gh == 1)),
                             reads=[B_U, B_mat], writes=[Bp])
                pv4 = view(pb, [2, 2, 128])
                if fwd:
                    for gh in range(2):
                        S.op("act", lambda e, pv4=pv4, gh=gh, q=q: e.activation(out=yft[:, gh * 32 + q * 2: gh * 32 + q * 2 + 2, :], in_=pv4[:, :, gh, :],
                                                                                func=AF.Copy), reads=[Bp], writes=[B_yft], par=True)
                else:
                    ty = tmpy[q % 2]; Bty = B_tmpy[q % 2]
                    S.op("act", lambda e, pv4=pv4, ty=ty: e.activation(out=ty, in_=pv4, func=AF.Copy), reads=[Bp], writes=[Bty])
                    for gh in range(2):
                        g0 = gh * 32 + q * 2
                        S.op("pool", lambda e, ty=ty, gh=gh, g0=g0: e.tensor_tensor(out=ty[:, :, gh, :], in0=ty[:, :, gh, :], in1=yft[:, g0:g0 + 2, :], op=ALU.add),
                             reads=[Bty, B_yft], writes=[Bty])
                        S.op("pool", lambda e, ty=ty, gh=gh, g0=g0: e.tensor_tensor(out=ty[:, :, gh, :], in0=ty[:, :, gh, :], in1=hd[:, g0:g0 + 2, :], op=ALU.add),
                             reads=[Bty, B_hd], writes=[Bty])
                        outap = bass.AP(gt.tensor, gt.offset + 16 * g0, [list(gt.ap[0]), [16, 2], [D, 8], [1, 16]])
                        S.op("act", lambda e, ty=ty, gh=gh, outap=outap: e.activation(out=outap, in_=ty[:, :, gh, :].rearrange("p a (t j) -> p a t j", t=8),
                                                                                      func=AF.Gelu), reads=[Bty], writes=[B_gt], par=True)
            if fwd:
                S.dma("sp", lambda e, t=t: e.dma_start(out=k.yf[t * 128:(t + 1) * 128, :], in_=yft.rearrange("p a b -> p (a b)")),
                      B_yft, reads=[B_yft], writes=[B_yf])
            else:
                dst = bass.AP(k.gout.tensor, t * 1024 * D, [[8 * D, 128], [1, 8 * D]])
                S.dma("sp", lambda e, dst=dst: e.dma_start(out=dst, in_=gt.rearrange("p a b -> p (a b)")), B_gt, reads=[B_gt], writes=[k.B_gout])
    S.barrier(); A.reset()
    phase_glu(k)


def phase_glu(k):
    nc, S, A = k.nc, k.S, k.A
    TT = 256; NTILE = k.NT // TT
    Wg = A.alloc([8, 2048], BF16); B_Wg = S.buf("Wg")
    load_w_bf16(k, Wg, B_Wg, k.w['s5_w_glu'][0], 8, 2048)
    grow = [A.alloc([D], F32) for _ in range(2)]; B_mv = S.bufs("mvg", 2)
    xt = [A.alloc([2, D], F32) for _ in range(2)]; B_xt = S.bufs("xtg", 2)
    gn = [A.alloc([2, D], BF16) for _ in range(2)]; B_gn = S.bufs("gn", 2)
    gT = A.alloc([8, TT], BF16); B_gT = S.buf("gT")
    sg = [A.alloc([512], F32) for _ in range(2)]; B_sg = S.bufs("sg", 2)
    t1 = [A.alloc([512], F32) for _ in range(2)]; B_t1 = S.bufs("t1", 2)
    psTv = [view(k.psT[i], [4, 256]) for i in range(2)]
    cur_seg = -1
    for t in range(NTILE):
        sl = t % 2
        seg = (t * TT) // k.SEG
        if seg != cur_seg:
            cur_seg = seg; b = seg % 2
            load_mod_rows(k, 0, seg, 2, grow[b], B_mv[b])
        rows = k.x[t * TT:(t + 1) * TT, :].rearrange("(g p) d -> p g d", p=128)
        S.dma("sp", lambda e, sl=sl, rows=rows: e.dma_start(out=xt[sl], in_=rows), B_xt[sl], writes=[B_xt[sl]])
        grows = k.gout[t * TT:(t + 1) * TT, :].rearrange("(g p) d -> p g d", p=128)
        S.dma("sp", lambda e, sl=sl, grows=grows: e.dma_start(out=gn[sl], in_=grows), B_gn[sl], reads=[k.B_gout], writes=[B_gn[sl]])
        for g in range(2):
            for kt in range(8):
                S.op("pe", lambda e, g=g, kt=kt, sl=sl: e.transpose(out=psTv[kt // 4][:, kt % 4, g * 128:(g + 1) * 128],
                                                                    in_=gn[sl][:, g, kt * 128:(kt + 1) * 128], identity=k.ident),
                     reads=[B_gn[sl], k.B_ident], writes=[k.BpsT[kt // 4]])
        S.op("dve", lambda e: e.tensor_copy(out=gT[:, 0:4, :], in_=psTv[0]), reads=[k.BpsT[0]], writes=[B_gT], par=True)
        S.op("act", lambda e: e.activation(out=gT[:, 4:8, :], in_=psTv[1], func=AF.Copy), reads=[k.BpsT[1]], writes=[B_gT], par=True)
        for g in range(2):
            for h in range(2):
                pv = k.ps[h]; Bv = k.Bps[h]
                pg = k.ps[2 + h]; Bg = k.Bps[2 + h]
                for (pb, Bp, c0) in ((pv, Bv, h * 512), (pg, Bg, 1024 + h * 512)):
                    for kt in range(8):
                        S.op("pe", lambda e, pb=pb, kt=kt, g=g, c0=c0: e.matmul(pb[:, :], lhsT=gT[:, kt, g * 128:(g + 1) * 128],
                                                                                rhs=Wg[:, kt, c0:c0 + 512], start=(kt == 0), stop=(kt == 7)),
                             reads=[B_gT, B_Wg], writes=[Bp])
                i2 = (g * 2 + h) % 2
                S.op("act", lambda e, pg=pg, i2=i2: e.activation(out=sg[i2], in_=pg, func=AF.Sigmoid), reads=[Bg], writes=[B_sg[i2]])
                S.op("dve", lambda e, pv=pv, i2=i2: e.tensor_tensor(out=t1[i2], in0=pv, in1=sg[i2], op=ALU.mult),
                     reads=[Bv, B_sg[i2]], writes=[B_t1[i2]])
                S.op("pool", lambda e, i2=i2, b=b, h=h: e.tensor_tensor(out=t1[i2], in0=t1[i2], in1=grow[b][:, h * 512:(h + 1) * 512], op=ALU.mult),
                     reads=[B_t1[i2], B_mv[b]], writes=[B_t1[i2]])
                S.op("pool", lambda e, sl=sl, g=g, h=h, i2=i2: e.tensor_tensor(out=xt[sl][:, g, h * 512:(h + 1) * 512],
                                                                              in0=xt[sl][:, g, h * 512:(h + 1) * 512], in1=t1[i2], op=ALU.add),
                     reads=[B_t1[i2], B_xt[sl]], writes=[B_xt[sl]])
        orow = k.xres[t * TT:(t + 1) * TT, :].rearrange("(g p) d -> p g d", p=128)
        S.dma("sp", lambda e, sl=sl, orow=orow: e.dma_start(out=orow, in_=xt[sl]), B_xt[sl], reads=[B_xt[sl]], writes=[k.B_xres])


def phase_gla(k):
    nc, S, A = k.nc, k.S, k.A
    NT = k.NT; TT = 256; NTILE = NT // TT; l = 1
    CH = 128; NCH = NT // CH; CPS = k.SEG // CH
    dr = lambda n, s: nc.dram_tensor(n, s, BF16).ap()
    qTd = dr("g_qT", [512, NT]); kTd = dr("g_kT", [512, NT]); ktd = dr("g_kt", [NT, 512]); vtd = dr("g_vt", [NT, 1024])
    srd = dr("g_sr", [NT, 1024]); gtd = dr("g_gt", [2, NT, 512]); ofd = dr("g_of", [NT, 1024]); gad = dr("g_ga", [NT, 1024])
    B_proj = S.buf("gproj"); B_of = S.buf("gof"); B_ga = S.buf("gga")
    k.gla_d = (qTd, kTd, ktd, vtd, srd, gtd, ofd, gad); k.gla_B = (B_proj, B_of, B_ga)
    _gla_g1(k)
    S.barrier(); A.reset()
    _gla_g2(k)
    S.barrier(); A.reset()
    _gla_g3(k)


def _gla_g1(k):
    nc, S, A = k.nc, k.S, k.A
    NT = k.NT; TT = 256; NTILE = NT // TT; l = 1
    CH = 128; NCH = NT // CH; CPS = k.SEG // CH
    qTd, kTd, ktd, vtd, srd, gtd, ofd, gad = k.gla_d
    B_proj, B_of, B_ga = k.gla_B
    psTv = [view(k.psT[i], [4, 256]) for i in range(2)]
    Win = A.alloc([8, 3072], BF16); B_Win = S.buf("Win")
    load_w_bf16(k, Win, B_Win, k.w['gla_w_in'][0], 8, 3072)
    Wa1 = A.alloc([8, 2, 16], BF16); Wa2 = A.alloc([2, 512], BF16); barow = A.alloc([2, 512], F32)
    B_wa = S.buf("wa")
    for j in range(2):
        for kt in range(8):
            S.dma("pool", lambda e, j=j, kt=kt: e.dma_start(out=Wa1[:, kt, j, :], in_=k.w['gla_w_a1'][0, j, kt * 128:(kt + 1) * 128, :]), B_wa, writes=[B_wa])
        S.dma("pool", lambda e, j=j: e.dma_start(out=Wa2[0:16, j, :], in_=k.w['gla_w_a2'][0, j, :, :]), B_wa, writes=[B_wa])
    S.dma("sp", lambda e: e.dma_start(out=barow, in_=bass.AP(k.w['gla_b_a'].tensor, 0, [[0, 128], [1, 1024]])), B_wa, writes=[B_wa])
    wcol = [A.alloc([8], F32) for _ in range(2)]; shcol = [A.alloc([8], F32) for _ in range(2)]
    B_mv = S.bufs("mvq", 2)
    xt = [A.alloc([2, D], F32) for _ in range(2)]; B_xt = S.bufs("xtq", 2)
    xn = A.alloc([2, D], BF16); B_xn = S.buf("xnq")
    junk = A.alloc([D], BF16); B_junk = S.buf("junkq")
    ss = [A.alloc([4], F32) for _ in range(2)]; B_ss = S.bufs("ssq", 2)
    hT = A.alloc([8, TT], BF16); B_hT = S.buf("hTq")
    qst = A.alloc([8, TT], BF16); B_qst = S.buf("qst")
    tok = [A.alloc([2, 512], BF16) for _ in range(5)]; B_tok = S.bufs("tokst", 5)
    uT = [A.alloc([TT], BF16) for _ in range(2)]; B_uT = S.bufs("uT", 2)
    zz = A.alloc([512], F32); za = A.alloc([512], F32); zt = A.alloc([512], F32)
    B_z = S.buf("zz")
    gst = A.alloc([2, 2, 512], BF16); B_gst = S.buf("gst")
    psTv = [view(k.psT[i], [4, 256]) for i in range(2)]
    cur_seg = -1
    for t in range(NTILE):
        sl = t % 2
        seg = (t * TT) // k.SEG
        if seg != cur_seg:
            cur_seg = seg; b = seg % 2
            load_mod_cols(k, l, seg, 1, wcol[b], B_mv[b])
            load_mod_cols(k, l, seg, 0, shcol[b], B_mv[b])
        rows = k.xres[t * TT:(t + 1) * TT, :].rearrange("(g p) d -> p g d", p=128)
        S.dma("sp", lambda e, sl=sl, rows=rows: e.dma_start(out=xt[sl], in_=rows), B_xt[sl], reads=[k.B_xres], writes=[B_xt[sl]])
        for g in range(2):
            rms_stats(k, xt[sl][:, g, :], ss[sl][:, g:g + 1], ss[sl][:, 2 + g:3 + g], junk, B_xt[sl], B_ss[sl], B_junk)
            S.op("act", lambda e, sl=sl, g=g: e.activation(out=xn[:, g, :], in_=xt[sl][:, g, :], func=AF.Copy, scale=ss[sl][:, 2 + g:3 + g]),
                 reads=[B_xt[sl], B_ss[sl]], writes=[B_xn], par=True)
        for g in range(2):
            for kt in range(8):
                S.op("pe", lambda e, g=g, kt=kt: e.transpose(out=psTv[kt // 4][:, kt % 4, g * 128:(g + 1) * 128],
                                                             in_=xn[:, g, kt * 128:(kt + 1) * 128], identity=k.ident),
                     reads=[B_xn, k.B_ident], writes=[k.BpsT[kt // 4]])
        for kt in range(8):
            if kt // 4 == 0:
                S.op("dve", lambda e, kt=kt, b=b: e.tensor_scalar(out=hT[:, kt, :], in0=psTv[0][:, kt % 4, :], scalar1=wcol[b][:, kt:kt + 1],
                                                                  scalar2=shcol[b][:, kt:kt + 1], op0=ALU.mult, op1=ALU.add),
                     reads=[k.BpsT[0], B_mv[b]], writes=[B_hT], par=True)
            else:
                S.op("act", lambda e, kt=kt, b=b: e.activation(out=hT[:, kt, :], in_=psTv[1][:, kt % 4, :], func=AF.Identity,
                                                               scale=wcol[b][:, kt:kt + 1], bias=shcol[b][:, kt:kt + 1]),
                     reads=[k.BpsT[1], B_mv[b]], writes=[B_hT], par=True)
        for mp in range(4):
            pb = k.ps[mp % 2]; Bp = k.Bps[mp % 2]
            for mi in range(2):
                m = mp * 2 + mi
                for kt in range(8):
                    S.op("pe", lambda e, pb=pb, mi=mi, m=m, kt=kt: e.matmul(pb[:, mi * 256:(mi + 1) * 256], lhsT=Win[:, kt, m * 128:(m + 1) * 128],
                                                                            rhs=hT[:, kt, :], start=(kt == 0), stop=(kt == 7)),
                         reads=[B_Win, B_hT], writes=[Bp])
            sc_ = (128.0 ** -0.5) if mp < 2 else 1.0
            S.op("act", lambda e, pb=pb, mp=mp, sc_=sc_: e.activation(out=qst[:, 2 * mp:2 * mp + 2, :], in_=view(pb, [2, 256]), func=AF.Copy, scale=sc_),
                 reads=[Bp], writes=[B_qst], par=True)
        S.dma("sp", lambda e, t=t: e.dma_start(out=bass.AP(qTd.tensor, t * TT, [[NT, 128], [128 * NT, 4], [1, TT]]), in_=qst[:, 0:4, :]),
              B_qst, reads=[B_qst], writes=[B_proj])
        S.dma("sp", lambda e, t=t: e.dma_start(out=bass.AP(kTd.tensor, t * TT, [[NT, 128], [128 * NT, 4], [1, TT]]), in_=qst[:, 4:8, :]),
              B_qst, reads=[B_qst], writes=[B_proj])
        for ci in range(5):
            c0 = 512 + ci * 512
            for g in range(2):
                isr = ci >= 3
                bi = 4 if isr else 2 + (ci * 2 + g) % 2
                pb = k.ps[bi]; Bp = k.Bps[bi]
                for kt in range(8):
                    S.op("pe", lambda e, pb=pb, kt=kt, g=g, c0=c0: e.matmul(pb[:, :], lhsT=hT[:, kt, g * 128:(g + 1) * 128], rhs=Win[:, kt, c0:c0 + 512],
                                                                            start=(kt == 0), stop=(kt == 7)), reads=[B_hT, B_Win], writes=[Bp])
                if isr:
                    S.op("act", lambda e, pb=pb, ci=ci, g=g: e.activation(out=tok[ci][:, g, :], in_=pb, func=AF.Silu), reads=[Bp], writes=[B_tok[ci]], par=True)
                else:
                    S.op("dve", lambda e, pb=pb, ci=ci, g=g: e.tensor_copy(out=tok[ci][:, g, :], in_=pb), reads=[Bp], writes=[B_tok[ci]], par=True)
        def tokdst(dten, width, c0):
            return bass.AP(dten.tensor, t * TT * width + c0, [[width, 128], [128 * width, 2], [1, 512]])
        S.dma("sp", lambda e, d_=tokdst(ktd, 512, 0): e.dma_start(out=d_, in_=tok[0]), B_tok[0], reads=[B_tok[0]], writes=[B_proj])
        for hh in range(2):
            S.dma("sp", lambda e, hh=hh, d_=tokdst(vtd, 1024, hh * 512): e.dma_start(out=d_, in_=tok[1 + hh]), B_tok[1 + hh], reads=[B_tok[1 + hh]], writes=[B_proj])
            S.dma("sp", lambda e, hh=hh, d_=tokdst(srd, 1024, hh * 512): e.dma_start(out=d_, in_=tok[3 + hh]), B_tok[3 + hh], reads=[B_tok[3 + hh]], writes=[B_proj])
        for j in range(2):
            pb = k.ps[5]; Bp = k.Bps[5]
            for kt in range(8):
                S.op("pe", lambda e, pb=pb, kt=kt, j=j: e.matmul(pb[0:16, 0:256], lhsT=Wa1[:, kt, j, :], rhs=hT[:, kt, :], start=(kt == 0), stop=(kt == 7)),
                     reads=[B_wa, B_hT], writes=[Bp])
            S.op("dve", lambda e, pb=pb, j=j: e.tensor_copy(out=uT[j][0:16, :], in_=pb[0:16, 0:256]), reads=[Bp], writes=[B_uT[j]])
            for g in range(2):
                S.op("pe", lambda e, pb=pb, j=j, g=g: e.matmul(pb[:, :], lhsT=uT[j][0:16, g * 128:(g + 1) * 128], rhs=Wa2[0:16, j, :], start=True, stop=True),
                     reads=[B_uT[j], B_wa], writes=[Bp])
                S.op("dve", lambda e, pb=pb, j=j: e.tensor_tensor(out=zz, in0=pb, in1=barow[:, j, :], op=ALU.add), reads=[Bp, B_wa, B_z], writes=[B_z])
                S.op("dve", lambda e: e.scalar_tensor_tensor(out=za, in0=zz, scalar=-1.0, in1=zz, op0=ALU.mult, op1=ALU.max), reads=[B_z], writes=[B_z])
                S.op("act", lambda e: e.activation(out=za, in_=za, func=AF.Exp, scale=-1.0), reads=[B_z], writes=[B_z])
                S.op("act", lambda e: e.activation(out=za, in_=za, func=AF.Ln, bias=1.0), reads=[B_z], writes=[B_z])
                S.op("dve", lambda e: e.tensor_scalar(out=zt, in0=zz, scalar1=0.0, scalar2=1.0 / 16, op0=ALU.min, op1=ALU.mult), reads=[B_z], writes=[B_z])
                S.op("dve", lambda e, j=j, g=g: e.scalar_tensor_tensor(out=gst[:, j, g, :], in0=za, scalar=-1.0 / 16, in1=zt, op0=ALU.mult, op1=ALU.add),
                     reads=[B_z], writes=[B_gst])
        for j in range(2):
            S.dma("sp", lambda e, j=j, d_=bass.AP(gtd.tensor, j * NT * 512 + t * TT * 512, [[512, 128], [128 * 512, 2], [1, 512]]): e.dma_start(out=d_, in_=gst[:, j]),
                  B_gst, reads=[B_gst], writes=[B_proj])


def _gla_g2(k):
    nc, S, A = k.nc, k.S, k.A
    NT = k.NT; TT = 256; NTILE = NT // TT; l = 1
    CH = 128; NCH = NT // CH; CPS = k.SEG // CH
    qTd, kTd, ktd, vtd, srd, gtd, ofd, gad = k.gla_d
    B_proj, B_of, B_ga = k.gla_B
    psTv = [view(k.psT[i], [4, 256]) for i in range(2)]
    onesf = A.alloc([128], F32); B_tri = S.buf("tri")
    tri = {n: A.alloc([128], BF16) for n in ("Lf", "Lb", "Tf", "Tb")}
    trif = A.alloc([128], F32)
    S.op("pool", lambda e: e.memset(onesf, 1.0), writes=[B_tri])
    for (n, pat, base, cm) in (("Lf", 1, 0, -1), ("Lb", -1, 0, 1), ("Tf", -1, -1, 1), ("Tb", 1, -1, -1)):
        S.op("pool", lambda e, pat=pat, base=base, cm=cm: e.affine_select(out=trif, in_=onesf, pattern=[[pat, 128]], compare_op=ALU.is_ge, fill=0.0,
                                                                          base=base, channel_multiplier=cm), reads=[B_tri], writes=[B_tri])
        S.op("pool", lambda e, n=n: e.tensor_copy(out=tri[n], in_=trif), reads=[B_tri], writes=[B_tri])
    NB = 2
    qc = [A.alloc([4, 128], BF16) for _ in range(NB)]; kc = [A.alloc([4, 128], BF16) for _ in range(NB)]
    ktk = [A.alloc([512], BF16) for _ in range(NB)]; vtk = [A.alloc([1024], BF16) for _ in range(NB)]; gtk = [A.alloc([512], BF16) for _ in range(NB)]
    B_ld = S.bufs("gld", NB)
    eb = A.alloc([4, 128], F32); enb = A.alloc([4, 128], F32); etl = A.alloc([512], F32)
    B_eb = S.buf("eb"); B_etl = S.buf("etl")
    qd = A.alloc([4, 128], BF16); kd = A.alloc([4, 128], BF16); ktl = A.alloc([512], BF16)
    B_qd = S.buf("qd"); B_kd = S.buf("kd"); B_ktl = S.buf("ktl")
    scm = A.alloc([4, 128], BF16); B_scm = S.buf("scm")
    Sst = A.alloc([4, 256], F32); Sb = A.alloc([4, 256], BF16); B_S = S.buf("Sst"); B_Sb = S.buf("Sb")
    ebl = A.alloc([4], F32); B_ebl = S.buf("ebl")
    ofs = A.alloc([1024], BF16); B_ofs = S.buf("ofs")
    osb = A.alloc([1024], F32); B_osb = S.buf("osb")
    ofl = A.alloc([1024], BF16); srl = A.alloc([1024], BF16); B_ofl = S.buf("ofl")
    oss = A.alloc([8], F32); B_oss = S.buf("oss")
    ngrow = A.alloc([1024], F32); B_ng = S.buf("ng")
    gat = A.alloc([1024], BF16); B_gat = S.buf("gat")
    junk2 = A.alloc([256], BF16); B_j2 = S.buf("junk2")
    S.dma("sp", lambda e: e.dma_start(out=ngrow, in_=bass.AP(k.w['gla_norm_g'].tensor, 0, [[0, 128], [1, 1024]])), B_ng, writes=[B_ng])
    it = 0
    for d in range(2):
        fwd = (d == 0)
        L = tri["Lf"] if fwd else tri["Lb"]; T = tri["Tf"] if fwd else tri["Tb"]
        lastc = 127 if fwd else 0
        S.op("dve", lambda e: e.memset(Sst, 0.0), writes=[B_S], small=True)
        S.op("pool", lambda e: e.memset(Sb, 0.0), writes=[B_Sb])
        chunks = list(range(NCH)) if fwd else list(range(NCH - 1, -1, -1))
        for ci, c in enumerate(chunks):
            sl = it % NB; it += 1
            seg = c // CPS
            t0 = c * CH
            S.dma("sp", lambda e, sl=sl, t0=t0: e.dma_start(out=qc[sl], in_=bass.AP(qTd.tensor, t0, [[NT, 128], [128 * NT, 4], [1, CH]])), B_ld[sl], reads=[B_proj], writes=[B_ld[sl]])
            S.dma("sp", lambda e, sl=sl, t0=t0: e.dma_start(out=kc[sl], in_=bass.AP(kTd.tensor, t0, [[NT, 128], [128 * NT, 4], [1, CH]])), B_ld[sl], reads=[B_proj], writes=[B_ld[sl]])
            S.dma("sp", lambda e, sl=sl, t0=t0: e.dma_start(out=ktk[sl], in_=ktd[t0:t0 + CH, :]), B_ld[sl], reads=[B_proj], writes=[B_ld[sl]])
            S.dma("sp", lambda e, sl=sl, t0=t0: e.dma_start(out=vtk[sl], in_=vtd[t0:t0 + CH, :]), B_ld[sl], reads=[B_proj], writes=[B_ld[sl]])
            S.dma("sp", lambda e, sl=sl, t0=t0, d=d: e.dma_start(out=gtk[sl], in_=gtd[d, t0:t0 + CH, :]), B_ld[sl], reads=[B_proj], writes=[B_ld[sl]])
            at_start = (c % CPS == 0) if fwd else (c % CPS == CPS - 1)
            if ci > 0 and at_start:
                kidx = seg if fwd else seg + 1
                S.op("dve", lambda e, kidx=kidx: e.tensor_scalar(out=Sst, in0=Sst, scalar1=k.keepb[:, kidx:kidx + 1], scalar2=None, op0=ALU.mult),
                     reads=[B_S, k.B_keep], writes=[B_S])
                S.op("act", lambda e: e.activation(out=Sb, in_=Sst, func=AF.Copy), reads=[B_S], writes=[B_Sb])
            S.op("pe", lambda e, sl=sl, T=T: e.matmul(k.ps[0][:, :], lhsT=T, rhs=gtk[sl], start=True, stop=True), reads=[B_tri, B_ld[sl]], writes=[k.Bps[0]])
            pv1 = view(k.ps[1], [4, 128])
            for h in range(4):
                S.op("pe", lambda e, sl=sl, h=h, L=L: e.matmul(pv1[:, h, :], lhsT=gtk[sl][:, h * 128:(h + 1) * 128], rhs=L, start=True, stop=True),
                     reads=[B_tri, B_ld[sl]], writes=[k.Bps[1]])
            S.op("act", lambda e: e.activation(out=etl, in_=k.ps[0], func=AF.Exp), reads=[k.Bps[0]], writes=[B_etl])
            S.op("act", lambda e: e.activation(out=eb, in_=pv1, func=AF.Exp), reads=[k.Bps[1]], writes=[B_eb])
            S.op("act", lambda e: e.activation(out=enb, in_=pv1, func=AF.Exp, scale=-1.0), reads=[k.Bps[1]], writes=[B_eb])
            S.op("act", lambda e, lastc=lastc: e.activation(out=ebl, in_=eb[:, :, lastc], func=AF.Copy), reads=[B_eb], writes=[B_ebl])
            S.op("pool", lambda e, sl=sl: e.tensor_tensor(out=qd, in0=qc[sl], in1=eb, op=ALU.mult), reads=[B_ld[sl], B_eb], writes=[B_qd])
            S.op("pool", lambda e, sl=sl: e.tensor_tensor(out=kd, in0=kc[sl], in1=enb, op=ALU.mult), reads=[B_ld[sl], B_eb], writes=[B_kd])
            S.op("pool", lambda e, sl=sl: e.tensor_tensor(out=ktl, in0=ktk[sl], in1=etl, op=ALU.mult), reads=[B_ld[sl], B_etl], writes=[B_ktl])
            pv2 = view(k.ps[2], [4, 128])
            for h in range(4):
                S.op("pe", lambda e, h=h: e.matmul(pv2[:, h, :], lhsT=kd[:, h, :], rhs=qd[:, h, :], start=True, stop=True), reads=[B_kd, B_qd], writes=[k.Bps[2]])
            mask_b = bass.AP(L.tensor, L.offset, [list(L.ap[0]), [0, 4], [1, 128]])
            S.op("dve", lambda e, mask_b=mask_b: e.tensor_tensor(out=scm, in0=pv2, in1=mask_b, op=ALU.mult), reads=[k.Bps[2], B_tri], writes=[B_scm])
            for h in range(4):
                pb = k.ps[3 + h // 2]; Bp = k.Bps[3 + h // 2]
                oc = (h % 2) * 256
                S.op("pe", lambda e, pb=pb, oc=oc, h=h, sl=sl: e.matmul(pb[:, oc:oc + 256], lhsT=scm[:, h, :], rhs=vtk[sl][:, h * 256:(h + 1) * 256], start=True, stop=False),
                     reads=[B_scm, B_ld[sl]], writes=[Bp])
                S.op("pe", lambda e, pb=pb, oc=oc, h=h: e.matmul(pb[:, oc:oc + 256], lhsT=qd[:, h, :], rhs=Sb[:, h, :], start=False, stop=True),
                     reads=[B_qd, B_Sb], writes=[Bp])
            if fwd:
                for hb in range(2):
                    S.op("act", lambda e, hb=hb: e.activation(out=ofs[:, hb * 512:(hb + 1) * 512], in_=k.ps[3 + hb], func=AF.Copy), reads=[k.Bps[3 + hb]], writes=[B_ofs], par=True)
                S.dma("sp", lambda e, t0=t0: e.dma_start(out=ofd[t0:t0 + CH, :], in_=ofs), B_ofs, reads=[B_ofs], writes=[B_of])
            else:
                S.dma("sp", lambda e, t0=t0: e.dma_start(out=ofl, in_=ofd[t0:t0 + CH, :]), B_ofl, reads=[B_of], writes=[B_ofl])
                S.dma("sp", lambda e, t0=t0: e.dma_start(out=srl, in_=srd[t0:t0 + CH, :]), B_ofl, reads=[B_proj], writes=[B_ofl])
                for hb in range(2):
                    S.op("act", lambda e, hb=hb: e.activation(out=osb[:, hb * 512:(hb + 1) * 512], in_=k.ps[3 + hb], func=AF.Copy), reads=[k.Bps[3 + hb]], writes=[B_osb], par=True)
                S.op("pool", lambda e: e.tensor_tensor(out=osb, in0=osb, in1=ofl, op=ALU.add), reads=[B_osb, B_ofl], writes=[B_osb])
                for h in range(4):
                    S.op("act", lambda e, h=h: e.activation(out=junk2, in_=osb[:, h * 256:(h + 1) * 256], func=AF.Square, accum_out=oss[:, h:h + 1]),
                         reads=[B_osb], writes=[B_j2, B_oss])
                S.op("dve", lambda e: e.tensor_scalar(out=oss[:, 0:4], in0=oss[:, 0:4], scalar1=1.0 / 256, scalar2=EPS, op0=ALU.mult, op1=ALU.add),
                     reads=[B_oss], writes=[B_oss], small=True)
                S.op("act", lambda e: e.activation(out=oss[:, 0:4], in_=oss[:, 0:4], func=AF.Sqrt), reads=[B_oss], writes=[B_oss], small=True)
                S.op("dve", lambda e: e.reciprocal(out=oss[:, 4:8], in_=oss[:, 0:4]), reads=[B_oss], writes=[B_oss], small=True)
                for h in range(4):
                    S.op("dve", lambda e, h=h: e.scalar_tensor_tensor(out=osb[:, h * 256:(h + 1) * 256], in0=osb[:, h * 256:(h + 1) * 256], scalar=oss[:, 4 + h:5 + h],
                                                                      in1=ngrow[:, h * 256:(h + 1) * 256], op0=ALU.mult, op1=ALU.mult),
                         reads=[B_osb, B_oss, B_ng], writes=[B_osb])
                S.op("pool", lambda e: e.tensor_tensor(out=gat, in0=osb, in1=srl, op=ALU.mult), reads=[B_osb, B_ofl], writes=[B_gat])
                S.dma("sp", lambda e, t0=t0: e.dma_start(out=gad[t0:t0 + CH, :], in_=gat), B_gat, reads=[B_gat], writes=[B_ga])
            for rnd in range(2):
                pb = k.ps[5]; Bp = k.Bps[5]
                for hh in range(2):
                    h = rnd * 2 + hh
                    S.op("pe", lambda e, pb=pb, hh=hh, h=h, sl=sl: e.matmul(pb[:, hh * 256:(hh + 1) * 256], lhsT=ktl[:, h * 128:(h + 1) * 128],
                                                                            rhs=vtk[sl][:, h * 256:(h + 1) * 256], start=True, stop=True),
                         reads=[B_ktl, B_ld[sl]], writes=[Bp])
                for hh in range(2):
                    h = rnd * 2 + hh
                    S.op("dve", lambda e, pb=pb, hh=hh, h=h: e.scalar_tensor_tensor(out=Sst[:, h, :], in0=Sst[:, h, :], scalar=ebl[:, h:h + 1],
                                                                                    in1=pb[:, hh * 256:(hh + 1) * 256], op0=ALU.mult, op1=ALU.add),
                         reads=[B_S, B_ebl, Bp], writes=[B_S])
            S.op("act", lambda e: e.activation(out=Sb, in_=Sst, func=AF.Copy), reads=[B_S], writes=[B_Sb])


def _gla_g3(k):
    nc, S, A = k.nc, k.S, k.A
    NT = k.NT; TT = 256; NTILE = NT // TT; l = 1
    CH = 128; NCH = NT // CH; CPS = k.SEG // CH
    qTd, kTd, ktd, vtd, srd, gtd, ofd, gad = k.gla_d
    B_proj, B_of, B_ga = k.gla_B
    psTv = [view(k.psT[i], [4, 256]) for i in range(2)]
    Wo = A.alloc([8, D], BF16); B_Wo = S.buf("Wo")
    load_w_bf16(k, Wo, B_Wo, k.w['gla_w_out'][0], 8, D)
    grow = [A.alloc([D], F32) for _ in range(2)]; B_mv2 = S.bufs("mvo", 2)
    xt = [A.alloc([2, D], F32) for _ in range(2)]; B_xt = S.bufs("xto", 2)
    gn = [A.alloc([2, D], BF16) for _ in range(2)]; B_gn = S.bufs("gno", 2)
    gT = A.alloc([8, TT], BF16); B_gT = S.buf("gTo")
    t1 = [A.alloc([512], F32) for _ in range(2)]; B_t1 = S.bufs("t1o", 2)
    cur_seg = -1
    for t in range(NTILE):
        sl = t % 2
        seg = (t * TT) // k.SEG
        if seg != cur_seg:
            cur_seg = seg; b = seg % 2
            load_mod_rows(k, l, seg, 2, grow[b], B_mv2[b])
        rows = k.xres[t * TT:(t + 1) * TT, :].rearrange("(g p) d -> p g d", p=128)
        S.dma("sp", lambda e, sl=sl, rows=rows: e.dma_start(out=xt[sl], in_=rows), B_xt[sl], reads=[k.B_xres], writes=[B_xt[sl]])
        grows = gad[t * TT:(t + 1) * TT, :].rearrange("(g p) d -> p g d", p=128)
        S.dma("sp", lambda e, sl=sl, grows=grows: e.dma_start(out=gn[sl], in_=grows), B_gn[sl], reads=[B_ga], writes=[B_gn[sl]])
        for g in range(2):
            for kt in range(8):
                S.op("pe", lambda e, g=g, kt=kt, sl=sl: e.transpose(out=psTv[kt // 4][:, kt % 4, g * 128:(g + 1) * 128],
                                                                    in_=gn[sl][:, g, kt * 128:(kt + 1) * 128], identity=k.ident),
                     reads=[B_gn[sl], k.B_ident], writes=[k.BpsT[kt // 4]])
        S.op("dve", lambda e: e.tensor_copy(out=gT[:, 0:4, :], in_=psTv[0]), reads=[k.BpsT[0]], writes=[B_gT], par=True)
        S.op("act", lambda e: e.activation(out=gT[:, 4:8, :], in_=psTv[1], func=AF.Copy), reads=[k.BpsT[1]], writes=[B_gT], par=True)
        for g in range(2):
            for h in range(2):
                bi = (g * 2 + h) % 4
                pb = k.ps[bi]; Bp = k.Bps[bi]
                for kt in range(8):
                    S.op("pe", lambda e, pb=pb, kt=kt, g=g, h=h: e.matmul(pb[:, :], lhsT=gT[:, kt, g * 128:(g + 1) * 128], rhs=Wo[:, kt, h * 512:(h + 1) * 512],
                                                                          start=(kt == 0), stop=(kt == 7)), reads=[B_gT, B_Wo], writes=[Bp])
                i2 = (g * 2 + h) % 2
                S.op("dve", lambda e, pb=pb, i2=i2, b=b, h=h: e.tensor_tensor(out=t1[i2], in0=pb, in1=grow[b][:, h * 512:(h + 1) * 512], op=ALU.mult),
                     reads=[Bp, B_mv2[b]], writes=[B_t1[i2]])
                S.op("pool", lambda e, sl=sl, g=g, h=h, i2=i2: e.tensor_tensor(out=xt[sl][:, g, h * 512:(h + 1) * 512],
                                                                              in0=xt[sl][:, g, h * 512:(h + 1) * 512], in1=t1[i2], op=ALU.add),
                     reads=[B_t1[i2], B_xt[sl]], writes=[B_xt[sl]])
        orow = k.xres[t * TT:(t + 1) * TT, :].rearrange("(g p) d -> p g d", p=128)
        S.dma("sp", lambda e, sl=sl, orow=orow: e.dma_start(out=orow, in_=xt[sl]), B_xt[sl], reads=[B_xt[sl]], writes=[k.B_xres])


_SEG, _NSEG = 2048, 8
_CACHE = {}


def _get_program():
    if "nc" not in _CACHE:
        flags = {"s5": True, "gla": True}
        nc, k = build_program(_SEG, _NSEG, flags)
        phase_mod(k)
        k.S.barrier(); k.A.reset()
        phase_s5(k)
        k.S.barrier(); k.A.reset()
        phase_mlp(k, 0, k.xres, k.B_xres, final=False)
        k.S.barrier(); k.A.reset()
        phase_gla(k)
        k.S.barrier(); k.A.reset()
        phase_mlp(k, 1, k.xres, k.B_xres, final=True)
        finish_program(k)
        _CACHE["nc"] = nc
    return _CACHE["nc"]


def kernel(**inputs):
    inp = {n: np.ascontiguousarray(np.asarray(v, dtype=np.float32)) for n, v in inputs.items()}
    nc = _get_program()
    NT = _SEG * _NSEG
    xp = inp["x_prompt"]; xs = inp["x_sample"]
    maps = []
    for c in range(8):
        if c < 2:
            x = xp[c]; cv = np.repeat(inp["c_prompt"][c:c + 1], _NSEG, 0); keep = np.ones((1, 8), np.float32)
        elif c == 2:
            x = xs.reshape(NT, D); cv = inp["c_sample"]; keep = np.zeros((1, 8), np.float32)
        else:
            x = np.zeros((NT, D), np.float32); cv = np.zeros((_NSEG, D), np.float32); keep = np.zeros((1, 8), np.float32)
        m = {"x": np.ascontiguousarray(x), "cvec": np.ascontiguousarray(cv), "keep": keep}
        for n in WNAMES:
            m[n] = inp[n]
        maps.append(m)
    res = run_bass_kernel_spmd(nc, maps, core_ids=list(range(8)))
    r = res.results
    y_prompt = np.stack([np.asarray(r[0]["out"], np.float32), np.asarray(r[1]["out"], np.float32)], 0)
    y_sample = np.asarray(r[2]["out"], np.float32).reshape(8, _SEG, D)
    return (y_prompt, y_sample)
```
